# Optimizing a Trainium2 kernel written in Bass

```python
import jax, jax.numpy as jnp
from jax import lax
import numpy as np

D_MODEL = 2048
BATCH = 8
SEQ = 4096
DEPTH = 2
DEC_BATCH = 16
DEC_SEQ = 64
PAST_LEN = 2048

CHUNK = 64
C_CONV = D_MODEL // 2
C_RET = D_MODEL // 2
D_MIX = C_CONV + C_RET
N_RET_HEADS = 8
RET_HEAD_DIM = C_RET // N_RET_HEADS
CONV_WIDTH = 31
CONV_STATE = CONV_WIDTH - 1
D_IN = 3 * C_CONV + 4 * C_RET
ROPE_BASE = 10000.0
EPS = 1e-6

kernel_name = "hybrid_conformer_conv_retention_stream_step"


def rms_norm(x, w):
    xf = x.astype(jnp.float32)
    y = xf * lax.rsqrt(jnp.mean(xf * xf, axis=-1, keepdims=True) + EPS)
    return (y * w.astype(jnp.float32)).astype(x.dtype)


def layer_norm(x, w, b):
    xf = x.astype(jnp.float32)
    mu = jnp.mean(xf, axis=-1, keepdims=True)
    xc = xf - mu
    var = jnp.mean(xc * xc, axis=-1, keepdims=True)
    y = xc * lax.rsqrt(var + EPS) * w.astype(jnp.float32) + b.astype(jnp.float32)
    return y.astype(x.dtype)


def rotary(x, pos):
    half = x.shape[-1] // 2
    inv_freq = ROPE_BASE ** (-jnp.arange(half, dtype=jnp.float32) / half)
    ang = pos.astype(jnp.float32)[:, None] * inv_freq[None, :]
    cos = jnp.cos(ang)[None, :, None, :]
    sin = jnp.sin(ang)[None, :, None, :]
    x1, x2 = x[..., :half], x[..., half:]
    return jnp.concatenate([x1 * cos - x2 * sin, x1 * sin + x2 * cos], axis=-1)


def retention_log_decay():
    return jnp.log1p(-jnp.exp2(-5.0 - jnp.arange(N_RET_HEADS, dtype=jnp.float32)))


def retention_block(R, qkv, log_g):
    q, k, v = qkv
    L = q.shape[2]
    i = jnp.arange(L, dtype=jnp.float32)
    lg = log_g[:, None]
    dmat = jnp.exp(jnp.abs(i[:, None] - i[None, :])[None] * log_g[:, None, None])
    s = jnp.einsum('bhid,bhjd->bhij', q, k) * dmat[None]
    intra = jnp.einsum('bhij,bhje->bhie', s, v)
    q_dec = q * jnp.exp((i + 1.0)[None, :] * lg)[None, :, :, None]
    inter = jnp.einsum('bhid,bhde->bhie', q_dec, R)
    k_dec = k * jnp.exp((L - 1.0 - i)[None, :] * lg)[None, :, :, None]
    R_new = R * jnp.exp(L * log_g)[None, :, None, None] + jnp.einsum('bhjd,bhje->bhde', k_dec, v)
    return R_new, intra + inter


def retention(q, k, v, R0, log_g):
    B, T, H, d = q.shape
    L = min(T, CHUNK)
    n = T // L

    def to_blocks(a):
        return a.reshape(B, n, L, H, d).transpose(1, 0, 3, 2, 4)

    R, o = lax.scan(lambda R, xs: retention_block(R, xs, log_g), R0,
                    (to_blocks(q), to_blocks(k), to_blocks(v)))
    o = o.transpose(1, 0, 3, 2, 4).reshape(B, T, H, d)
    return o, R


def causal_dwconv(u, buf, w, b):
    full = jnp.concatenate([buf.astype(u.dtype), u], axis=1)
    y = lax.conv_general_dilated(full, w[:, None, :].astype(u.dtype), window_strides=(1,),
                                 padding='VALID', dimension_numbers=('NWC', 'WIO', 'NWC'),
                                 feature_group_count=u.shape[-1])
    return y + b.astype(u.dtype), full[:, -CONV_STATE:]


def mixer(h, pos, conv_buf, R0, w_in, conv_w, conv_b, ln_w, ln_b, w_pw, gn_w, w_out, log_g):
    B, T, _ = h.shape
    z = h @ w_in
    a, a_gate, g_conv, q, k, v, g_ret = jnp.split(
        z, [C_CONV, 2 * C_CONV, 3 * C_CONV, 3 * C_CONV + C_RET,
            3 * C_CONV + 2 * C_RET, 3 * C_CONV + 3 * C_RET], axis=-1)
    u = a * jax.nn.sigmoid(a_gate)
    c, new_buf = causal_dwconv(u, conv_buf, conv_w, conv_b)
    c = jax.nn.silu(layer_norm(c, ln_w, ln_b)) @ w_pw
    y_conv = jax.nn.silu(g_conv) * c
    def heads(t):
        return t.reshape(B, T, N_RET_HEADS, RET_HEAD_DIM).astype(jnp.float32)
    qh = rotary(heads(q), pos)
    kh = rotary(heads(k), pos) * (RET_HEAD_DIM ** -0.5)
    vh = heads(v)
    o, R = retention(qh, kh, vh, R0, log_g)
    mu = jnp.mean(o, axis=-1, keepdims=True)
    oc = o - mu
    o = oc * lax.rsqrt(jnp.mean(oc * oc, axis=-1, keepdims=True) + EPS)
    o = o.reshape(B, T, C_RET) * gn_w.astype(jnp.float32)
    y_ret = jax.nn.silu(g_ret) * o.astype(h.dtype)
    y = jnp.concatenate([y_conv, y_ret], axis=-1) @ w_out
    return y, new_buf, R


def trunk(x, pos, conv_bufs, R0s, norm_pre, w_in, conv_w, conv_b, conv_ln_w, conv_ln_b,
          w_pw, ret_gn_w, w_out, norm_post):
    log_g = retention_log_decay()
    bufs, states = [], []
    for l in range(DEPTH):
        h = rms_norm(x, norm_pre[l])
        y, buf, R = mixer(h, pos, conv_bufs[l], R0s[l].astype(jnp.float32), w_in[l], conv_w[l],
                          conv_b[l], conv_ln_w[l], conv_ln_b[l], w_pw[l], ret_gn_w[l],
                          w_out[l], log_g)
        x = x + rms_norm(y, norm_post[l])
        bufs.append(buf)
        states.append(R)
    return x, jnp.stack(bufs), jnp.stack(states)


def setup_inputs(seed: int = 0) -> dict:
    key = jax.random.key(seed)
    ks = jax.random.split(key, 14)
    f32 = jnp.float32
    nrm = lambda k, s: jax.random.normal(k, s, dtype=f32)
    return {
        "x_prompt": nrm(ks[0], (BATCH, SEQ, D_MODEL)),
        "x_sample": nrm(ks[1], (DEC_BATCH, DEC_SEQ, D_MODEL)),
        "cache_conv": nrm(ks[2], (DEPTH, DEC_BATCH, CONV_STATE, C_CONV)) * 0.5,
        "state_ret": nrm(ks[3], (DEPTH, DEC_BATCH, N_RET_HEADS, RET_HEAD_DIM, RET_HEAD_DIM)),
        "norm_pre": 1.0 + 0.01 * nrm(ks[4], (DEPTH, D_MODEL)),
        "w_in": nrm(ks[5], (DEPTH, D_MODEL, D_IN)) * D_MODEL ** -0.5,
        "conv_w": nrm(ks[6], (DEPTH, CONV_WIDTH, C_CONV)) * CONV_WIDTH ** -0.5,
        "conv_b": 0.01 * nrm(ks[7], (DEPTH, C_CONV)),
        "conv_ln_w": 1.0 + 0.01 * nrm(ks[8], (DEPTH, C_CONV)),
        "conv_ln_b": 0.01 * nrm(ks[9], (DEPTH, C_CONV)),
        "w_pw": nrm(ks[10], (DEPTH, C_CONV, C_CONV)) * C_CONV ** -0.5,
        "ret_gn_w": 1.0 + 0.01 * nrm(ks[11], (DEPTH, C_RET)),
        "w_out": nrm(ks[12], (DEPTH, D_MIX, D_MODEL)) * D_MIX ** -0.5,
        "norm_post": 1.0 + 0.01 * nrm(ks[13], (DEPTH, D_MODEL)),
    }


def reference(x_prompt, x_sample, cache_conv, state_ret, norm_pre, w_in, conv_w, conv_b,
              conv_ln_w, conv_ln_b, w_pw, ret_gn_w, w_out, norm_post):
    B, T, _ = x_prompt.shape
    zero_bufs = jnp.zeros((DEPTH, B, CONV_STATE, C_CONV), dtype=x_prompt.dtype)
    zero_R = jnp.zeros((DEPTH, B, N_RET_HEADS, RET_HEAD_DIM, RET_HEAD_DIM), dtype=jnp.float32)
    pos_prompt = jnp.arange(T)
    y_prompt, new_conv_prompt, new_ret_prompt = trunk(
        x_prompt, pos_prompt, zero_bufs, zero_R, norm_pre, w_in, conv_w, conv_b, conv_ln_w,
        conv_ln_b, w_pw, ret_gn_w, w_out, norm_post)
    pos_sample = PAST_LEN + jnp.arange(x_sample.shape[1])
    y_sample, new_conv_sample, new_ret_sample = trunk(
        x_sample, pos_sample, cache_conv, state_ret, norm_pre, w_in, conv_w, conv_b, conv_ln_w,
        conv_ln_b, w_pw, ret_gn_w, w_out, norm_post)
    return (y_prompt, y_sample, new_conv_prompt, new_ret_prompt, new_conv_sample, new_ret_sample)
```

```python
import numpy as np
from contextlib import ExitStack
import concourse.bass as bass
import concourse.mybir as mybir
from concourse.bass_utils import run_bass_kernel_spmd

F32 = mybir.dt.float32
BF16 = mybir.dt.bfloat16
I32 = mybir.dt.int32
AF = mybir.ActivationFunctionType
ALU = mybir.AluOpType
AX = mybir.AxisListType

D = 2048
C = 1024
H = 8
HD = 128
CW = 31
CS = 30
SEQ = 4096
DS = 64
PAST = 2048
EPS = 1e-6
NT = 256
NSLAB = 38
SLABW = 4096
NSLOT = 3
UW = CS + NT
NG = 4
P_NPRE = 0
P_CB = 16
P_LNW = 24
P_LNB = 32
P_GNW = 40
P_CW = 48
NPV = 48 + 8 * CW
K_MASK = 0
K_ODEC = 1024
K_ODEC2 = 1032
K_KD128 = 1040
K_KD64 = 1048
K_IDF = 1056
K_ONES = 1184
NK = 1312
GAM = [1.0 - 2.0 ** (-5.0 - h) for h in range(H)]


PSUM_KEYS = {"G0", "G1", "O0", "O1", "T0", "T1", "SU0", "SU1"}


class Sched:
    ENGS = ["pe", "dve", "act", "pool", "sp"]

    def __init__(self):
        self.ops = []
        self.lastw = {}
        self.readers = {}

    def add(self, eng, fn, reads=(), writes=(), dma_sem=None, ndma=0):
        if getattr(self, "cap", None) is not None:
            self.cap.append((eng, fn, list(reads), list(writes), dma_sem, ndma))
            return -1
        idx = len(self.ops)
        deps = {}
        for k in reads:
            w = self.lastw.get(k)
            if w is not None:
                deps[w] = True
            if k in PSUM_KEYS:
                for r in self.readers.get(k, ()):
                    if self.ops[r]["eng"] != eng:
                        deps.setdefault(r, False)
        for k in writes:
            w = self.lastw.get(k)
            if w is not None:
                deps.setdefault(w, False)
            for r in self.readers.get(k, ()):
                deps.setdefault(r, False)
        for k in writes:
            self.lastw[k] = idx
            self.readers[k] = []
        for k in reads:
            if k not in writes:
                self.readers.setdefault(k, []).append(idx)
        self.ops.append(dict(eng=eng, fn=fn, deps=deps, dma_sem=dma_sem, ndma=ndma, signal=False))
        return idx

    def replay(self, lst, n):
        for _ in range(min(n, len(lst))):
            self.add(*lst.pop(0))

    def emit(self, nc, block, es):
        ops = self.ops
        for op in ops:
            for d in list(op["deps"].keys()):
                raw = op["deps"][d]
                dop = ops[d]
                same = dop["eng"] == op["eng"]
                is_dma = dop["dma_sem"] is not None or op["dma_sem"] is not None
                if same and not is_dma and op["eng"] == "pe":
                    del op["deps"][d]
                    continue
                dop["signal"] = True
        cnt = {}
        semnames = set()
        for op in ops:
            if op["dma_sem"] is not None:
                s = "d_" + op["dma_sem"]
                cnt[s] = cnt.get(s, 0) + 16 * op["ndma"]
                op["sem"], op["count"] = s, cnt[s]
                semnames.add(s)
            elif op["signal"]:
                s = "e_" + op["eng"]
                cnt[s] = cnt.get(s, 0) + 1
                op["sem"], op["count"] = s, cnt[s]
                semnames.add(s)
        semh = {s: es.enter_context(nc.semaphore(s)) for s in sorted(semnames)}
        self.nsem = len(semh)

        def run(eng, e):
            waited = {}
            for op in ops:
                if op["eng"] != eng:
                    continue
                need = {}
                for d in op["deps"]:
                    dop = ops[d]
                    s = dop["sem"]
                    if dop["count"] > need.get(s, 0):
                        need[s] = dop["count"]
                for s, c in need.items():
                    if waited.get(s, 0) >= c:
                        continue
                    e.wait_ge(semh[s], c)
                    waited[s] = c
                if op["fn"] is None:
                    continue
                r = op["fn"](e)
                if op["dma_sem"] is not None:
                    assert len(r) == op["ndma"], (len(r), op["ndma"])
                    for ins in r:
                        ins.then_inc(semh[op["sem"]], 16)
                elif op["signal"]:
                    r.then_inc(semh[op["sem"]], 1)

        @block.tensor
        def _(e):
            run("pe", e)

        @block.vector
        def _(e):
            run("dve", e)

        @block.scalar
        def _(e):
            run("act", e)

        @block.gpsimd
        def _(e):
            run("pool", e)

        @block.sync
        def _(e):
            run("sp", e)


class Tile:
    def __init__(self, kind, idx=0):
        self.kind = kind
        self.idx = idx
        if kind == "p":
            self.ntok = NT
            self.nsub = 2
            self.nseg = 1
            self.Ls = NT
            self.sc = [(0, 128, 0), (128, 128, 0)]
            self.L = 128
            self.pos0 = idx * NT
            self.need_u32 = idx == SEQ // NT - 1
            self.last = idx == SEQ // NT - 1
        else:
            self.ntok = 128
            self.nsub = 1
            self.nseg = 2
            self.Ls = DS
            self.sc = [(0, DS, 1), (DS, DS, 0)]
            self.L = DS
            self.pos0 = PAST
            self.need_u32 = True
            self.last = True


def build_program(cfg=None):
    cfg = cfg or {}
    nc = bass.Bass("TRN2", target_bir_lowering=False)

    def din(name, shape, dt=F32):
        return nc.dram_tensor(name, list(shape), dt, kind="ExternalInput").ap()

    def dout(name, shape):
        return nc.dram_tensor(name, list(shape), F32, kind="ExternalOutput").ap()

    xp = din("xp", [SEQ, D])
    xs = din("xs", [128, D])
    wall = din("wall", [2, NSLAB, 128, SLABW])
    pvec = din("pvec", [2, 128, NPV])
    wpost = din("wpost", [2, 128, D])
    ccT = din("ccT", [2, 2, 128, 8, CS])
    sret = din("sret", [2, 2, H, HD, HD])
    ktab = din("ktab", [128, NK])
    cstab = din("cstab", [2, 128, SEQ])
    cbf = din("cbf", [128, 256])
    wb = nc.dram_tensor("wb", [2, NSLAB, 128, SLABW], BF16, kind="Internal").ap()
    yp = dout("yp", [SEQ, D])
    ys = dout("ys", [128, D])
    ncp = dout("ncp", [2, CS, C])
    nrp = dout("nrp", [2, H, HD, HD])
    ncs = dout("ncs", [2, 2, CS, C])
    nrs = dout("nrs", [2, 2, H, HD, HD])

    S = Sched()
    with ExitStack() as es:
        def sb(name, shape, dt):
            return es.enter_context(nc.sbuf_tensor(name, list(shape), dt))

        def ps(name, shape, dt):
            return es.enter_context(nc.psum_tensor(name, list(shape), dt))

        xbuf = [sb("xbuf0", [128, 2, D], F32), sb("xbuf1", [128, 2, D], F32)]
        htm = sb("htm", [128, 2, D], BF16)
        hT = sb("hT", [128, 16, NT], BF16)
        ymix = sb("ymix", [128, 16, NT], BF16)
        ubuf = [sb(f"ubuf{l}", [128, 8, UW], BF16) for l in range(2)]
        u32 = [sb(f"u32_{g}", [128, 8, 32], F32) for g in range(2)]
        ucT = sb("ucT", [32, C], F32)
        sgc = sb("sgc", [128, 8, NT], BF16)
        sgr = sb("sgr", [128, 8, NT], BF16)
        qT = sb("qT", [128, 8, NT], BF16)
        kT = sb("kT", [128, 8, NT], BF16)
        vtm = sb("vtm", [128, 2, C], BF16)
        cf32 = sb("cf32", [128, 8, NT], F32)
        tb = [sb(f"tb{s}", [128, D], F32) for s in range(2)]
        th = [sb(f"th{i}", [128, NT], F32) for i in range(2)]
        csq = [sb(f"csq{i}", [128, NT], F32) for i in range(2)]
        cn = [sb(f"cn{i}", [128, NT], F32) for i in range(2)]
        rotA = [sb(f"rotA{i}", [128, NT], BF16) for i in range(2)]
        rotB = [sb(f"rotB{i}", [128, NT], BF16) for i in range(2)]
        lmu = sb("lmu", [128, NT], F32)
        lvar = sb("lvar", [128, NT], F32)
        lrs = sb("lrs", [128, NT], F32)
        ltmp = sb("ltmp", [128, NT], F32)
        lnmr = sb("lnmr", [128, NT], F32)
        actc = sb("actc", [128, 8, NT], BF16)
        Sm = sb("Sm", [128, 8, 128], BF16)
        kdtm = sb("kdtm", [128, 8, 128], BF16)
        ontm = sb("ontm", [128, 8, 128], BF16)
        R = [[sb(f"R{l}_0", [128, 8, 128], F32),
              xbuf[1][:, 1, l * 1024:(l + 1) * 1024].rearrange("p (h e) -> p h e", h=8)] for l in range(2)]
        Rbf1 = sb("Rbf", [128, 8, 128], BF16)
        Rbf = [Rbf1, Rbf1]
        wslot = [sb(f"wslot{i}", [128, SLABW], BF16) for i in range(NSLOT)]
        diag = [sb(f"diag{i}", [128, CW, 128], BF16) for i in range(2)]
        cosb = sb("cosb", [128, NT], F32)
        sinb = sb("sinb", [128, NT], F32)
        kt = sb("kt", [128, NK], F32)
        cb16 = sb("cb16", [128, 256], BF16)
        pv = [sb(f"pv{l}", [128, NPV], F32) for l in range(2)]
        wpb1 = sb("wpb", [128, D], F32)
        wpb = [wpb1, wpb1]
        st = sb("st", [128, 64], F32)

        gbank = [ps(f"gb{i}", [128, 512], F32) for i in range(2)]
        TB32 = [ps("tbk0", [128, 512], F32), ps("tbk1", [128, 512], F32)]
        TB = [t[:, 0:256].bitcast(BF16) for t in TB32]
        SU = [ps(f"su{i}", [128, 512], F32) for i in range(2)]
        OB = [ps(f"ob{i}", [128, 512], F32) for i in range(2)]
        G = [gbank[0][:, 0:256], gbank[1][:, 0:256], OB[0][:, 0:256], OB[1][:, 0:256],
             TB32[0][:, 0:256], TB32[1][:, 0:256]]
        GKEY = ["G0", "G1", "O0", "O1", "T0", "T1"]
        ident = cb16[:, 0:128]
        perm = cb16[:, 128:256]
        identf = kt[:, K_IDF:K_IDF + 128]
        onesm = kt[:, K_ONES:K_ONES + 128]
        maskT = kt[:, K_MASK:K_MASK + 1024].rearrange("p (h i) -> p h i", h=8)

        state = dict(g=0, slab=0, rot=0, pool=[0, 1, 2, 3])
        deferred = []

        def run_deferred():
            while deferred:
                deferred.pop(0)()

        def nextg():
            pool = state["pool"]
            i = pool[state["g"] % len(pool)]
            state["g"] += 1
            return i

        ST_SSA, ST_VA, ST_RA, ST_TA = 0, 2, 4, 6
        ST_S1, ST_S2, ST_MEAN, ST_MSQ, ST_VAR, ST_RS, ST_T, ST_A, ST_NB = 8, 16, 24, 32, 40, 48, 56, 8, 16
        st2 = sb("st2", [128, 64], F32)
        ST2_SSQ, ST2_SS, ST2_V, ST2_R, ST2_T = 0, 16, 18, 20, 22
        ST2_A, ST2_NB = 32, 40

        def rsqrt(v, y, t, kv, ky, ktmp, iters=3):
            S.add("dve", lambda e: e.tensor_single_scalar(out=y.bitcast(I32), in_=v.bitcast(I32), scalar=1,
                                                          op=ALU.arith_shift_right), reads=[kv], writes=[ky])
            S.add("dve", lambda e: e.tensor_scalar(out=y.bitcast(I32), in0=y.bitcast(I32), scalar1=-1,
                                                   scalar2=1597463007, op0=ALU.mult, op1=ALU.add),
                  reads=[ky], writes=[ky])
            for _ in range(iters):
                S.add("dve", lambda e: e.scalar_tensor_tensor(out=t, in0=y, scalar=-0.5, in1=y, op0=ALU.mult,
                                                              op1=ALU.mult), reads=[ky], writes=[ktmp])
                S.add("dve", lambda e: e.tensor_tensor(out=t, in0=t, in1=v, op=ALU.mult), reads=[ktmp, kv],
                      writes=[ktmp])
                S.add("dve", lambda e: e.scalar_tensor_tensor(out=y, in0=t, scalar=1.5, in1=y, op0=ALU.add,
                                                              op1=ALU.mult), reads=[ktmp, ky], writes=[ky])

        S.add("pool", lambda e: [e.dma_start(out=cb16[:], in_=cbf)], writes=["cb16"], dma_sem="c0", ndma=1)
        S.add("sp", lambda e: [e.dma_start(out=kt[:], in_=ktab)], writes=["kt"], dma_sem="c1", ndma=1)
        S.add("sp", lambda e: [e.dma_start(out=pv[0][:], in_=pvec[0]), e.dma_start(out=pv[1][:], in_=pvec[1])],
              writes=["pv"], dma_sem="c2", ndma=2)
        first_use = set()

        def load_slab(l, i):
            k = state["slab"] % NSLOT
            state["slab"] += 1
            if (l, i) not in first_use:
                first_use.add((l, i))
                S.add("pool", lambda e: [e.dma_start(out=wslot[k][:], in_=wall[l, i])],
                      reads=(["kt", "pv", "cb16", "x0_0", "x0_1", "x1_0"] if len(first_use) == 1 else []),
                      writes=[f"ws{k}"], dma_sem=f"wsp{k}", ndma=1)
                S.add("sp", lambda e: [e.dma_start(out=wb[l, i], in_=wslot[k][:])], reads=[f"ws{k}"],
                      writes=[f"wb{l}_{i}"], dma_sem=f"wbk{k}", ndma=1)
            else:
                S.add("sp", lambda e: [e.dma_start(out=wslot[k][:], in_=wb[l, i])], reads=[f"wb{l}_{i}"],
                      writes=[f"ws{k}"], dma_sem=f"ws{k}", ndma=1)
            return k

        def stage_a(T, l):
            nsub = T.nsub
            P = pv[l]
            X = xbuf[T.par]
            xk = [f"x{T.par}_{s}" for s in range(2)]
            for s in range(nsub):
                S.add("act", lambda e, s=s: e.activation(out=htm[:, s, :], in_=X[:, s, :], func=AF.Square,
                                                         accum_out=st[:, ST_SSA + s:ST_SSA + s + 1]),
                      reads=[xk[s]], writes=[f"htm{s}", f"ssA{s}"])
            va = st[:, ST_VA:ST_VA + nsub]
            ra = st[:, ST_RA:ST_RA + nsub]
            ta = st[:, ST_TA:ST_TA + nsub]
            S.add("dve", lambda e: e.tensor_scalar(out=va, in0=st[:, ST_SSA:ST_SSA + nsub], scalar1=1.0 / D,
                                                   scalar2=EPS, op0=ALU.mult, op1=ALU.add),
                  reads=[f"ssA{s}" for s in range(nsub)], writes=["vA"])
            rsqrt(va, ra, ta, "vA", "rA", "tA", iters=2)
            for s in range(nsub):
                S.add("dve", lambda e, s=s: e.tensor_scalar(out=htm[:, s, :], in0=X[:, s, :],
                                                            scalar1=st[:, ST_RA + s:ST_RA + s + 1], scalar2=None,
                                                            op0=ALU.mult),
                      reads=[xk[s], "rA"], writes=[f"htm{s}"])

        def stage_a2(T, l):
            nsub = T.nsub
            P = pv[l]
            for s in range(nsub):
                for g in range(4):
                    hf = g % 2
                    tv = TB[hf][:, :]

                    def tr(e, s=s, g=g, tv=tv):
                        r = None
                        for j in range(4):
                            kc = 4 * g + j
                            r = e.transpose(tv[:, j * 128:(j + 1) * 128], htm[:, s, kc * 128:(kc + 1) * 128], ident)
                        return r
                    S.add("pe", tr, reads=[f"htm{s}", "cb16"], writes=[f"T{hf}"])
                    S.add("dve", lambda e, s=s, g=g, tv=tv: e.tensor_tensor(
                        out=hT[:, 4 * g:4 * g + 4, s * 128:(s + 1) * 128],
                        in0=tv.rearrange("p (a b) -> p a b", a=4),
                        in1=P[:, P_NPRE + 4 * g:P_NPRE + 4 * g + 4].unsqueeze(2).to_broadcast([128, 4, 128]),
                        op=ALU.mult), reads=[f"T{hf}", "pv"], writes=[f"hT{s}{g}"])


        def tile_layer(T, l, skip_a=False, hook1=None, hook2=None):
            N = T.ntok
            nsub = T.nsub
            nseg = T.nseg
            Ls = T.Ls
            P = pv[l]
            nsc = len(T.sc)
            state["pool"] = [0, 1, 2, 3]
            X = xbuf[T.par]
            xk = [f"x{T.par}_{s}" for s in range(2)]
            HTK = [f"hT{s}{g}" for s in range(nsub) for g in range(4)]
            UBK = [f"ub{l}_{c}" for c in range(8)]
            jk = [(rotA[0], "rA0"), (rotB[0], "rB0"), (rotA[1], "rA1"), (rotB[1], "rB1")]

            def nextjunk():
                state["jk"] = state.get("jk", 0) + 1
                return jk[state["jk"] % 4]

            if not skip_a:
                stage_a(T, l)
                stage_a2(T, l)

            def fm_group(k, cc, gi):
                wv = wslot[k][:].rearrange("p (kc c) -> p kc c", kc=16)

                def f(e):
                    r = None
                    for kc in range(16):
                        r = e.matmul(G[gi][:, :N], lhsT=wv[:, kc, cc * 128:(cc + 1) * 128], rhs=hT[:, kc, :N],
                                     start=(kc == 0), stop=(kc == 15))
                    return r
                S.add("pe", f, reads=[f"ws{k}"] + HTK, writes=[GKEY[gi]])

            pend = []

            def flush_rot(keep):
                while len(pend) > keep:
                    (which, h, r) = pend.pop(0)
                    gi = nextg()

                    def f(e, r=r, gi=gi):
                        e.matmul(G[gi][:, :N], lhsT=ident, rhs=rotA[r][:, :N], start=True, stop=False)
                        return e.matmul(G[gi][:, :N], lhsT=perm, rhs=rotB[r][:, :N], start=False, stop=True)
                    S.add("pe", f, reads=[f"rA{r}", f"rB{r}", "cb16"], writes=[GKEY[gi]])
                    dst = qT if which == "q" else kT
                    S.add("act", lambda e, gi=gi, dst=dst, h=h: e.activation(out=dst[:, h, :N], in_=G[gi][:, :N],
                                                                             func=AF.Copy),
                          reads=[GKEY[gi]], writes=[f"{which}{h}"])

            def seg_view(ap2d, lo):
                return ap2d[:, 0:nseg * (CS + Ls)].rearrange("p (g w) -> p g w", g=nseg)[:, :, lo:lo + Ls]

            def slab_gate_a(i):
                k = load_slab(l, i)
                is_gate = (i % 2 == 0)
                for cc in range(2):
                    c = 2 * (i // 2) + cc
                    gi = nextg()
                    fm_group(k, cc, gi)
                    if is_gate:
                        S.add("act", lambda e, gi=gi, c=c: e.activation(out=th[c % 2][:, :N], in_=G[gi][:, :N],
                                                                        func=AF.Tanh, scale=0.5),
                              reads=[GKEY[gi]], writes=[f"th{c % 2}"])
                        S.add("dve", lambda e, c=c: e.tensor_scalar(out=th[c % 2][:, :N], in0=th[c % 2][:, :N],
                                                                    scalar1=0.5, scalar2=0.5, op0=ALU.mult,
                                                                    op1=ALU.add),
                              reads=[f"th{c % 2}"], writes=[f"th{c % 2}"])
                    else:
                        S.add("dve", lambda e, gi=gi, c=c: e.tensor_tensor(
                            out=seg_view(ubuf[l][:, c, :], CS),
                            in0=G[gi][:, :N].rearrange("p (g w) -> p g w", g=nseg),
                            in1=th[c % 2][:, :N].rearrange("p (g w) -> p g w", g=nseg), op=ALU.mult),
                            reads=[GKEY[gi], f"th{c % 2}"], writes=[UBK[c]])
                        if T.need_u32:
                            for g in range(nseg):
                                hi = (g + 1) * Ls
                                S.add("dve", lambda e, gi=gi, c=c, g=g, hi=hi: e.tensor_tensor(
                                    out=u32[g][:, c, :], in0=G[gi][:, hi - 32:hi], in1=th[c % 2][:, hi - 32:hi],
                                    op=ALU.mult), reads=[GKEY[gi], f"th{c % 2}"], writes=[f"u32_{g}_{c}"])

            def slab_qk(i):
                k = load_slab(l, i)
                which = "q" if i < 12 else "k"
                for cc in range(2):
                    h = 2 * ((i - 8) % 4) + cc
                    gi = nextg()
                    fm_group(k, cc, gi)
                    r = state["rot"] % 2
                    state["rot"] += 1
                    S.add("dve", lambda e, gi=gi, r=r: e.tensor_tensor(out=rotA[r][:, :N], in0=G[gi][:, :N],
                                                                       in1=cosb[:, :N], op=ALU.mult),
                          reads=[GKEY[gi], "cos"], writes=[f"rA{r}"])
                    S.add("dve", lambda e, gi=gi, r=r: e.tensor_tensor(out=rotB[r][:, :N], in0=G[gi][:, :N],
                                                                       in1=sinb[:, :N], op=ALU.mult),
                          reads=[GKEY[gi], "cos"], writes=[f"rB{r}"])
                    pend.append((which, h, r))
                    flush_rot(1)

            def slab_v(i):
                k = load_slab(l, i)
                j = i - 16
                wv = wslot[k][:].rearrange("p (kc c) -> p kc c", kc=16)
                for sci, (c0, L, slot) in enumerate(T.sc):
                    gi = nextg()

                    def f(e, gi=gi, c0=c0, L=L, wv=wv):
                        r = None
                        for kc in range(16):
                            r = e.matmul(G[gi][:L, :], lhsT=hT[:, kc, c0:c0 + L], rhs=wv[:, kc, :],
                                         start=(kc == 0), stop=(kc == 15))
                        return r
                    S.add("pe", f, reads=[f"ws{k}"] + [f"hT{c0 // 128}{g}" for g in range(4)], writes=[GKEY[gi]])
                    S.add("act", lambda e, gi=gi, L=L, sci=sci, j=j: e.activation(
                        out=vtm[:L, sci, j * 256:(j + 1) * 256], in_=G[gi][:L, :], func=AF.Copy),
                        reads=[GKEY[gi]], writes=[f"v{sci}_{j}"])

            def slab_gate(i):
                k = load_slab(l, i)
                dst, nm = (sgc, "sgc") if i < 24 else (sgr, "sgr")
                for cc in range(2):
                    c = 2 * ((i - 20) % 4) + cc
                    gi = nextg()
                    fm_group(k, cc, gi)
                    S.add("act", lambda e, gi=gi, c=c, dst=dst: e.activation(out=dst[:, c, :N], in_=G[gi][:, :N],
                                                                             func=AF.Silu),
                          reads=[GKEY[gi]], writes=[f"{nm}{c}"])

            VK = [[f"v{sci}_{j}" for j in range(4)] for sci in range(2)]

            def build_diag(c):
                dg = diag[c % 2]
                for eng_, t0_, t1_ in (("pool", 0, 12), ("dve", 12, CW)):
                    S.add(eng_, lambda e, c=c, dg=dg, t0_=t0_, t1_=t1_: e.tensor_tensor(
                        out=dg[:, t0_:t1_, :], in0=ident.unsqueeze(1).to_broadcast([128, t1_ - t0_, 128]),
                        in1=P[:, P_CW + c * CW + t0_:P_CW + c * CW + t1_].unsqueeze(2).to_broadcast([128, t1_ - t0_, 128]),
                        op=ALU.mult), reads=["cb16", "pv"], writes=[f"dg{c % 2}_{eng_}"])

            for i in range(8):
                slab_gate_a(i)
                if i == 3:
                    build_diag(0)
                if i == 5:
                    build_diag(1)

            if T.need_u32:
                for g in range(nseg):
                    for pc in range(4):
                        gi = nextg()

                        def f(e, g=g, pc=pc, gi=gi):
                            e.matmul(G[gi][:32, 0:128], lhsT=u32[g][:, 2 * pc, :], rhs=identf, start=True, stop=True)
                            return e.matmul(G[gi][:32, 128:256], lhsT=u32[g][:, 2 * pc + 1, :], rhs=identf,
                                            start=True, stop=True)
                        S.add("pe", f, reads=[f"u32_{g}_{2 * pc}", f"u32_{g}_{2 * pc + 1}", "kt"], writes=[GKEY[gi]])
                        S.add("act", lambda e, gi=gi, pc=pc: e.activation(out=ucT[:32, pc * 256:(pc + 1) * 256],
                                                                          in_=G[gi][:32, :], func=AF.Copy),
                              reads=[GKEY[gi]], writes=[f"ucT{pc}"])
                    dst = ncp[l] if T.kind == "p" else ncs[l, g]
                    S.add("sp", lambda e, dst=dst: [e.dma_start(out=dst, in_=ucT[2:32, :])],
                          reads=[f"ucT{pc}" for pc in range(4)],
                          writes=[f"o_nc{T.kind}{l}{g}"], dma_sem="onc", ndma=1)

            run_deferred()
            gm, gq = 4, 5

            def stats(c):
                def f(e):
                    e.matmul(G[gm][:, :N], lhsT=onesm, rhs=cf32[:, c, :N], start=(c == 0), stop=(c == 7))
                    return e.matmul(G[gq][:, :N], lhsT=onesm, rhs=csq[c % 2][:, :N], start=(c == 0), stop=(c == 7))
                S.add("pe", f, reads=[f"cf{c}", f"csq{c % 2}", "kt"], writes=[GKEY[gm], GKEY[gq]])

            for c in range(8):
                dg = diag[c % 2]
                gi = nextg()

                def f(e, c=c, dg=dg, gi=gi):
                    r = None
                    for g in range(nseg):
                        uo = g * (CS + Ls)
                        for kk in range(CW):
                            r = e.matmul(G[gi][:, g * Ls:(g + 1) * Ls], lhsT=dg[:, kk, :],
                                         rhs=ubuf[l][:, c, uo + kk:uo + kk + Ls], start=(kk == 0), stop=(kk == CW - 1))
                    return r
                S.add("pe", f, reads=[f"dg{c % 2}_pool", f"dg{c % 2}_dve", UBK[c]], writes=[GKEY[gi]])
                if c + 2 < 8:
                    build_diag(c + 2)
                S.add("act", lambda e, c=c, gi=gi: e.activation(out=cf32[:, c, :N], in_=G[gi][:, :N], func=AF.Identity,
                                                                bias=P[:, P_CB + c:P_CB + c + 1]),
                      reads=[GKEY[gi], "pv"], writes=[f"cf{c}"])
                S.add("act", lambda e, c=c, gi=gi: e.activation(out=csq[c % 2][:, :N], in_=G[gi][:, :N], func=AF.Square,
                                                                bias=P[:, P_CB + c:P_CB + c + 1]),
                      reads=[GKEY[gi], "pv"], writes=[f"csq{c % 2}"])
                if c >= 1:
                    stats(c - 1)
            stats(7)
            S.add("act", lambda e: e.activation(out=lmu[:, :N], in_=G[gm][:, :N], func=AF.Copy), reads=[GKEY[gm]],
                  writes=["lmu"])
            S.add("dve", lambda e: e.tensor_tensor(out=ltmp[:, :N], in0=lmu[:, :N], in1=lmu[:, :N], op=ALU.mult),
                  reads=["lmu"], writes=["ltmp"])
            S.add("dve", lambda e: e.scalar_tensor_tensor(out=lvar[:, :N], in0=G[gq][:, :N], scalar=EPS, in1=ltmp[:, :N],
                                                          op0=ALU.add, op1=ALU.subtract),
                  reads=[GKEY[gq], "ltmp"], writes=["lvar"])
            if T.kind == "p" and not T.last:
                S.add("dve", lambda e: e.tensor_copy(out=ubuf[l][:, :, 0:CS], in_=ubuf[l][:, :, NT:NT + CS]),
                      reads=UBK, writes=UBK)
            rsqrt(lvar[:, :N], lrs[:, :N], ltmp[:, :N], "lvar", "lrs", "ltmp")
            S.add("dve", lambda e: e.scalar_tensor_tensor(out=lnmr[:, :N], in0=lmu[:, :N], scalar=-1.0, in1=lrs[:, :N],
                                                          op0=ALU.mult, op1=ALU.mult),
                  reads=["lmu", "lrs"], writes=["lnmr"])

            def ln_norm(c):
                S.add("dve", lambda e, c=c: e.tensor_tensor(out=cn[c % 2][:, :N], in0=cf32[:, c, :N], in1=lrs[:, :N],
                                                            op=ALU.mult), reads=[f"cf{c}", "lrs"], writes=[f"cn{c % 2}"])
                S.add("dve", lambda e, c=c: e.tensor_tensor(out=cn[c % 2][:, :N], in0=cn[c % 2][:, :N], in1=lnmr[:, :N],
                                                            op=ALU.add), reads=[f"cn{c % 2}", "lnmr"],
                      writes=[f"cn{c % 2}"])
                S.add("act", lambda e, c=c: e.activation(out=actc[:, c, :N], in_=cn[c % 2][:, :N], func=AF.Silu,
                                                         bias=P[:, P_LNB + c:P_LNB + c + 1],
                                                         scale=P[:, P_LNW + c:P_LNW + c + 1]),
                      reads=[f"cn{c % 2}", "pv"], writes=[f"ac{c}"])

            lnc = 0
            for i in range(16, 20):
                slab_v(i)
            for i in range(8, 16):
                slab_qk(i)
                if lnc < 8:
                    ln_norm(lnc)
                    lnc += 1
            flush_rot(0)

            def ret_scores(sci):
                c0, L, slot = T.sc[sci]
                for hb in range(2):
                    def f(e, hb=hb):
                        r = None
                        for hh in range(4):
                            h = 4 * hb + hh
                            r = e.matmul(SU[hb][:L, hh * 128:hh * 128 + L], lhsT=kT[:, h, c0:c0 + L],
                                         rhs=qT[:, h, c0:c0 + L], start=True, stop=True)
                        return r
                    S.add("pe", f, reads=[f"k{4 * hb + j}" for j in range(4)] + [f"q{4 * hb + j}" for j in range(4)],
                          writes=[f"SU{hb}"])
                    S.add("dve", lambda e, hb=hb: e.tensor_tensor(
                        out=Sm[:L, 4 * hb:4 * hb + 4, :L],
                        in0=SU[hb][:L, :].rearrange("p (h i) -> p h i", h=4)[:, :, :L],
                        in1=maskT[:L, 4 * hb:4 * hb + 4, :L], op=ALU.mult),
                        reads=[f"SU{hb}", "kt"], writes=[f"Sm{hb}"])

                def tr(e):
                    r = None
                    for h in range(8):
                        r = e.transpose(TB[h // 4][:L, (h % 4) * 128:(h % 4 + 1) * 128], kT[:, h, c0:c0 + L], ident)
                    return r
                S.add("pe", tr, reads=[f"k{h}" for h in range(8)] + ["cb16"], writes=["T0", "T1"])
                kdc = K_KD128 if L == 128 else K_KD64
                for h in range(8):
                    S.add("dve", lambda e, h=h: e.tensor_scalar(
                        out=kdtm[:L, h, :], in0=TB[h // 4][:L, (h % 4) * 128:(h % 4 + 1) * 128],
                        scalar1=kt[:L, kdc + h:kdc + h + 1], scalar2=None, op0=ALU.mult),
                        reads=[f"T{h // 4}", "kt"], writes=[f"kd{h}"])

            def ret_out(sci, chain=None):
                c0, L, slot = T.sc[sci]
                RK = [f"R{l}_{slot}_{h}" for h in range(8)]
                S.add("act", lambda e: e.activation(out=Rbf[l][:], in_=R[l][slot][:], func=AF.Copy),
                      reads=RK, writes=["Rbf"])
                for hb in range(2):
                    def f(e, hb=hb):
                        r = None
                        for hh in range(4):
                            h = 4 * hb + hh
                            e.matmul(OB[hb][:L, hh * 128:(hh + 1) * 128], lhsT=Sm[:L, h, :L],
                                     rhs=vtm[:L, sci, h * 128:(h + 1) * 128], start=True, stop=False)
                            r = e.matmul(OB[hb][:L, hh * 128:(hh + 1) * 128], lhsT=qT[:, h, c0:c0 + L],
                                         rhs=Rbf[l][:, h, :], start=False, stop=True)
                        return r
                    S.add("pe", f, reads=[f"Sm{hb}", "Rbf"] + VK[sci] + [f"q{4 * hb + j}" for j in range(4)],
                          writes=[f"O{hb}"])
                for hb in range(2):
                    def f(e, hb=hb):
                        r = None
                        for hh in range(4):
                            h = 4 * hb + hh
                            r = e.matmul(SU[hb][:, hh * 128:(hh + 1) * 128], lhsT=kdtm[:L, h, :],
                                         rhs=vtm[:L, sci, h * 128:(h + 1) * 128], start=True, stop=True)
                        return r
                    S.add("pe", f, reads=[f"kd{4 * hb + j}" for j in range(4)] + VK[sci], writes=[f"SU{hb}"])
                    for hh in range(4):
                        h = 4 * hb + hh
                        S.add("dve", lambda e, hb=hb, hh=hh, h=h: e.scalar_tensor_tensor(
                            out=R[l][slot][:, h, :], in0=R[l][slot][:, h, :], scalar=float(GAM[h] ** L),
                            in1=SU[hb][:, hh * 128:(hh + 1) * 128], op0=ALU.mult, op1=ALU.add),
                            reads=[RK[h], f"SU{hb}"], writes=[RK[h]])
                for hb in range(2):
                    S.add("dve", lambda e, hb=hb: e.tensor_reduce(
                        out=st[:L, ST_S1 + 4 * hb:ST_S1 + 4 * hb + 4],
                        in_=OB[hb][:L, :].rearrange("p (h e) -> p h e", h=4), axis=AX.X, op=ALU.add),
                        reads=[f"O{hb}"], writes=[f"s1_{hb}"])
                    for hh in range(4):
                        h = 4 * hb + hh
                        jt, jkey = nextjunk()
                        S.add("act", lambda e, hb=hb, hh=hh, h=h, jt=jt: e.activation(
                            out=jt[:L, 0:128], in_=OB[hb][:L, hh * 128:(hh + 1) * 128], func=AF.Square,
                            accum_out=st[:L, ST_S2 + h:ST_S2 + h + 1]), reads=[f"O{hb}"], writes=[f"s2_{h}", jkey])
                S.cap = chain
                mean = st[:L, ST_MEAN:ST_MEAN + 8]
                msq = st[:L, ST_MSQ:ST_MSQ + 8]
                var = st[:L, ST_VAR:ST_VAR + 8]
                rs = st[:L, ST_RS:ST_RS + 8]
                tt = st[:L, ST_T:ST_T + 8]
                aa = st2[:L, ST2_A:ST2_A + 8]
                nb = st2[:L, ST2_NB:ST2_NB + 8]
                S.add("dve", lambda e: e.tensor_scalar(out=mean, in0=st[:L, ST_S1:ST_S1 + 8], scalar1=1.0 / HD,
                                                       scalar2=None, op0=ALU.mult), reads=["s1_0", "s1_1"],
                      writes=["gmean"])
                S.add("dve", lambda e: e.tensor_tensor(out=msq, in0=mean, in1=mean, op=ALU.mult), reads=["gmean"],
                      writes=["gmsq"])
                S.add("dve", lambda e: e.scalar_tensor_tensor(out=var, in0=st[:L, ST_S2:ST_S2 + 8], scalar=1.0 / HD,
                                                              in1=msq, op0=ALU.mult, op1=ALU.subtract),
                      reads=[f"s2_{h}" for h in range(8)] + ["gmsq"], writes=["gvar"])
                S.add("dve", lambda e: e.tensor_tensor(out=var, in0=var, in1=kt[:L, K_ODEC2:K_ODEC2 + 8], op=ALU.mult),
                      reads=["gvar", "kt"], writes=["gvar"])
                S.add("dve", lambda e: e.tensor_scalar(out=var, in0=var, scalar1=EPS, scalar2=None, op0=ALU.add),
                      reads=["gvar"], writes=["gvar"])
                rsqrt(var, rs, tt, "gvar", "grs", "gtt")
                S.add("dve", lambda e: e.tensor_tensor(out=aa, in0=rs, in1=kt[:L, K_ODEC:K_ODEC + 8], op=ALU.mult),
                      reads=["grs", "kt"], writes=["ga"])
                S.add("dve", lambda e: e.scalar_tensor_tensor(out=nb, in0=mean, scalar=-1.0, in1=aa, op0=ALU.mult,
                                                              op1=ALU.mult), reads=["gmean", "ga"], writes=["gnb"])
                S.cap = None

            def ret_norm(sci):
                c0, L, slot = T.sc[sci]
                for hb in range(2):
                    for hh in range(4):
                        h = 4 * hb + hh
                        S.add("act", lambda e, hb=hb, hh=hh, h=h: e.activation(
                            out=ontm[:L, h, :], in_=OB[hb][:L, hh * 128:(hh + 1) * 128], func=AF.Identity,
                            bias=st2[:L, ST2_NB + h:ST2_NB + h + 1], scale=st2[:L, ST2_A + h:ST2_A + h + 1]),
                            reads=[f"O{hb}", "ga", "gnb"], writes=[f"on{h}"])

            def ret_fin(sci):
                c0, L, slot = T.sc[sci]

                def tr(e):
                    r = None
                    for h in range(8):
                        r = e.transpose(TB[h // 4][:, (h % 4) * 128:(h % 4) * 128 + L], ontm[:L, h, :], ident[:L, :L])
                    return r
                S.add("pe", tr, reads=[f"on{h}" for h in range(8)] + ["cb16"], writes=["T0", "T1"])
                for h in range(8):
                    S.add("dve", lambda e, h=h: e.scalar_tensor_tensor(
                        out=ymix[:, 8 + h, c0:c0 + L], in0=TB[h // 4][:, (h % 4) * 128:(h % 4) * 128 + L],
                        scalar=P[:, P_GNW + h:P_GNW + h + 1], in1=sgr[:, h, c0:c0 + L], op0=ALU.mult, op1=ALU.mult),
                        reads=[f"T{h // 4}", "pv", f"sgr{h}"], writes=[f"ym{c0 // 128}_{8 + h}_{sci}"])

            state["pool"] = [0, 1]
            ret_scores(0)
            ret_out(0)
            S.add("sp", lambda e: [e.dma_start(out=wpb[l][:], in_=wpost[l])], writes=["wpb"], dma_sem="c3", ndma=1)
            if hook1 is not None:
                hook1()
            state["pool"] = [0, 1, 4, 5]
            for i in range(24, 28):
                slab_gate(i)
            if nsc > 1:
                ret_scores(1)
            slab_gate(20)
            ret_norm(0)
            for i in range(21, 24):
                slab_gate(i)
            ret_fin(0)
            chain = []
            if nsc > 1:
                ret_out(1, chain)

            for j in range(2):
                k = load_slab(l, 28 + j)
                wv = wslot[k][:].rearrange("p (kc c) -> p kc c", kc=8)
                for mm in range(4):
                    m = 4 * j + mm
                    gi = nextg()

                    def f(e, wv=wv, mm=mm, gi=gi):
                        r = None
                        for kc in range(8):
                            r = e.matmul(G[gi][:, :N], lhsT=wv[:, kc, mm * 128:(mm + 1) * 128], rhs=actc[:, kc, :N],
                                         start=(kc == 0), stop=(kc == 7))
                        return r
                    S.add("pe", f, reads=[f"ws{k}"] + [f"ac{c}" for c in range(8)], writes=[GKEY[gi]])
                    S.add("dve", lambda e, m=m, gi=gi: e.tensor_tensor(out=ymix[:, m, :N], in0=G[gi][:, :N],
                                                                       in1=sgc[:, m, :N], op=ALU.mult),
                          reads=[GKEY[gi], f"sgc{m}"], writes=[f"ym{s_}_{m}" for s_ in range(nsub)])
                    S.replay(chain, 3)
            S.replay(chain, len(chain))
            if hook2 is not None:
                hook2()
            if nsc > 1:
                ret_norm(1)
                ret_fin(1)

            state["pool"] = [0, 1, 2, 3]

            def ymk(s):
                ks = [f"ym{s}_{m}" for m in range(8)]
                for sci, (c0, L, slot) in enumerate(T.sc):
                    if c0 // 128 == s:
                        ks += [f"ym{s}_{8 + h}_{sci}" for h in range(8)]
                return ks
            for cg in range(8):
                k = load_slab(l, 30 + cg)
                wv = wslot[k][:].rearrange("p (kc c) -> p kc c", kc=16)
                for s in range(nsub):
                    gi = nextg()

                    def f(e, wv=wv, s=s, gi=gi):
                        r = None
                        for kc in range(16):
                            r = e.matmul(G[gi][:, :], lhsT=ymix[:, kc, s * 128:(s + 1) * 128], rhs=wv[:, kc, :],
                                         start=(kc == 0), stop=(kc == 15))
                        return r
                    S.add("pe", f, reads=[f"ws{k}"] + ymk(s), writes=[GKEY[gi]])
                    S.add("dve", lambda e, s=s, cg=cg, gi=gi: e.tensor_tensor(
                        out=tb[s][:, cg * 256:(cg + 1) * 256], in0=G[gi][:, :], in1=wpb[l][:, cg * 256:(cg + 1) * 256],
                        op=ALU.mult), reads=[GKEY[gi], "wpb"], writes=[f"tb{s}_{cg}"])
                    jt, jkey = nextjunk()
                    S.add("act", lambda e, s=s, cg=cg, gi=gi, jt=jt: e.activation(
                        out=jt[:, :], in_=G[gi][:, :], func=AF.Square,
                        accum_out=st2[:, ST2_SSQ + 8 * s + cg:ST2_SSQ + 8 * s + cg + 1]),
                        reads=[GKEY[gi]], writes=[f"ssq{s}_{cg}", jkey])
            S.add("dve", lambda e: e.tensor_reduce(
                out=st2[:, ST2_SS:ST2_SS + nsub],
                in_=st2[:, ST2_SSQ:ST2_SSQ + 8 * nsub].rearrange("p (s c) -> p s c", s=nsub), axis=AX.X, op=ALU.add),
                reads=[f"ssq{s}_{cg}" for s in range(nsub) for cg in range(8)], writes=["ssE"])
            ve = st2[:, ST2_V:ST2_V + nsub]
            re_ = st2[:, ST2_R:ST2_R + nsub]
            te = st2[:, ST2_T:ST2_T + nsub]
            S.add("dve", lambda e: e.tensor_scalar(out=ve, in0=st2[:, ST2_SS:ST2_SS + nsub], scalar1=1.0 / D,
                                                   scalar2=EPS, op0=ALU.mult, op1=ALU.add), reads=["ssE"], writes=["vE"])
            rsqrt(ve, re_, te, "vE", "rE", "tE", iters=2)
            for s in range(nsub):
                TBK = [f"tb{s}_{cg}" for cg in range(8)]
                if l == 0:
                    S.add("dve", lambda e, s=s: e.scalar_tensor_tensor(
                        out=X[:, s, :], in0=tb[s][:], scalar=st2[:, ST2_R + s:ST2_R + s + 1], in1=X[:, s, :],
                        op0=ALU.mult, op1=ALU.add), reads=TBK + ["rE", xk[s]],
                        writes=[xk[s]] + ([f"R{ll}_1_{h}" for ll in range(2) for h in range(8)]
                                          if (T.kind == "p" and T.idx == 1 and s == 1) else []))
                else:
                    S.add("dve", lambda e, s=s: e.scalar_tensor_tensor(
                        out=tb[s][:], in0=tb[s][:], scalar=st2[:, ST2_R + s:ST2_R + s + 1], in1=X[:, s, :],
                        op0=ALU.mult, op1=ALU.add), reads=TBK + ["rE", xk[s]], writes=TBK)

        ptiles = [Tile("p", i) for i in range(SEQ // NT)]
        stile = Tile("s")
        for T in ptiles:
            T.par = T.idx % 2
        stile.par = 1
        tasks = []
        for k in range(0, len(ptiles), 2):
            A, B = ptiles[k], ptiles[k + 1]
            tasks += [(A, 0), (B, 0), (A, 1), (B, 1)]
        tasks = [(stile, 0), (stile, 1)] + tasks
        if "ntasks" in cfg:
            tasks = tasks[:cfg["ntasks"]]

        def load_x(T):
            X = xbuf[T.par]
            if T.kind == "s":
                S.add("sp", lambda e: [e.dma_start(out=X[:, 0, :], in_=xs)], writes=[f"x{T.par}_0"],
                      dma_sem=f"x{T.par}0", ndma=1)
            else:
                for s in range(T.nsub):
                    r0 = T.idx * NT + s * 128
                    extra = []
                    if T.idx == 1 and s == 1:
                        extra = [f"R{ll}_1_{h}" for ll in range(2) for h in range(8)]
                    S.add("sp", lambda e, s=s, r0=r0: [e.dma_start(out=X[:, s, :], in_=xp[r0:r0 + 128, :])],
                          writes=[f"x{T.par}_{s}"] + extra, dma_sem=f"x{T.par}{s}", ndma=1)

        def load_cos(T):
            if T.kind == "s":
                S.add("sp", lambda e: [e.dma_start(out=cosb[:, 0:DS], in_=cstab[0][:, PAST:PAST + DS]),
                                       e.dma_start(out=cosb[:, DS:2 * DS], in_=cstab[0][:, PAST:PAST + DS]),
                                       e.dma_start(out=sinb[:, 0:DS], in_=cstab[1][:, PAST:PAST + DS]),
                                       e.dma_start(out=sinb[:, DS:2 * DS], in_=cstab[1][:, PAST:PAST + DS])],
                      writes=["cos"], dma_sem="cos", ndma=4)
            else:
                p0 = T.pos0
                S.add("sp", lambda e, p0=p0: [e.dma_start(out=cosb[:, :], in_=cstab[0][:, p0:p0 + NT]),
                                              e.dma_start(out=sinb[:, :], in_=cstab[1][:, p0:p0 + NT])],
                      writes=["cos"], dma_sem="cos", ndma=2)

        load_x(stile)
        load_x(ptiles[0])
        for ll in range(2):
            S.add("sp", lambda e, ll=ll: [e.dma_start(out=R[ll][0][:], in_=sret[ll, 1].rearrange("h d e -> d h e"))],
                  writes=[f"R{ll}_0_{h}" for h in range(8)], dma_sem=f"r{ll}0", ndma=1)
            S.add("sp", lambda e, ll=ll: [e.dma_start(out=R[ll][1], in_=sret[ll, 0].rearrange("h d e -> d h e"))],
                  writes=[f"R{ll}_1_{h}" for h in range(8)] + ["x1_1"], dma_sem=f"r{ll}1", ndma=1)
            uo = [0, CS + DS]
            S.add("pool", lambda e, ll=ll, uo=uo: [e.dma_start(out=ubuf[ll][:, :, uo[b]:uo[b] + CS], in_=ccT[ll, b])
                                                   for b in range(2)],
                  writes=[f"ub{ll}_{c}" for c in range(8)], dma_sem=f"cc{ll}", ndma=2)
        hoisted = False
        for ti, (T, l) in enumerate(tasks):
            N = T.ntok
            load_cos(T)
            if T.kind == "p" and T.idx == 0 and l == 0:
                run_deferred()
                for ll in range(2):
                    S.add("dve", lambda e, ll=ll: e.memset(R[ll][0][:], 0.0), writes=[f"R{ll}_0_{h}" for h in range(8)])
                    S.add("dve", lambda e, ll=ll: e.memset(ubuf[ll][:, :, 0:CS], 0.0),
                          writes=[f"ub{ll}_{c}" for c in range(8)])
            nxt = tasks[ti + 1] if ti + 1 < len(tasks) else None
            hook1 = hook2 = None
            can_hoist = nxt is not None and nxt[0] is not T and cfg.get("hoist", True)
            if can_hoist:
                hook1 = (lambda nT=nxt[0], nl=nxt[1]: stage_a(nT, nl))
                hook2 = (lambda nT=nxt[0], nl=nxt[1]: stage_a2(nT, nl))
            tile_layer(T, l, skip_a=hoisted, hook1=hook1, hook2=hook2)
            hoisted = can_hoist
            def post(T=T, l=l, ti=ti):
                if l != 1:
                    return
                for s in range(T.nsub):
                    if T.kind == "s":
                        dst = ys
                    else:
                        r0 = T.idx * NT + s * 128
                        dst = yp[r0:r0 + 128, :]
                    S.add("sp", lambda e, s=s, dst=dst: [e.dma_start(out=dst, in_=tb[s][:])],
                          reads=[f"tb{s}_{cg}" for cg in range(8)],
                          writes=[f"o_y{ti}_{s}"], dma_sem=f"oy{s}", ndma=1)
                if T.last:
                    for ll in range(2):
                        if T.kind == "s":
                            for b in range(2):
                                S.add("sp", lambda e, ll=ll, b=b: [e.dma_start(out=nrs[ll, b].rearrange("h d e -> d h e"),
                                                                               in_=R[ll][1 - b][:])],
                                      reads=[f"R{ll}_{1 - b}_{h}" for h in range(8)], writes=[f"o_rs{ll}{b}"],
                                      dma_sem=f"orr{ll}{1 - b}", ndma=1)
                        else:
                            S.add("sp", lambda e, ll=ll: [e.dma_start(out=nrp[ll].rearrange("h d e -> d h e"),
                                                                      in_=R[ll][0][:])],
                                  reads=[f"R{ll}_0_{h}" for h in range(8)], writes=[f"o_rp{ll}"], dma_sem=f"orr{ll}0", ndma=1)
                if T.kind == "p":
                    if T.idx + 2 < len(ptiles):
                        load_x(ptiles[T.idx + 2])
                else:
                    load_x(ptiles[1])
            deferred.append(post)
        run_deferred()
        S.add("sp", None, reads=[], writes=[])
        fin = S.ops[-1]
        for i, op in enumerate(S.ops[:-1]):
            if op["dma_sem"] is not None and (op["dma_sem"].startswith("oy") or op["dma_sem"].startswith("orr") or op["dma_sem"] == "onc"):
                fin["deps"][i] = True

        block = es.enter_context(nc.Block())
        S.emit(nc, block, es)
    return nc, S


def _const_tables():
    ktab = np.zeros((128, NK), np.float64)
    i = np.arange(128)
    for h in range(H):
        g = GAM[h]
        ii = i[None, :]
        jj = i[:, None]
        same = (ii // 64) == (jj // 64)
        earlier = (jj // 64) < (ii // 64)
        dm = np.where(same, g ** np.abs(ii - jj), np.where(earlier, g ** (ii - jj).clip(0), 0.0))
        m = dm * g ** (-(ii + 1.0)) * HD ** -0.5
        ktab[:, K_MASK + h * 128:K_MASK + (h + 1) * 128] = m
        ktab[:, K_ODEC + h] = g ** (i + 1.0)
        ktab[:, K_ODEC2 + h] = g ** (2.0 * (i + 1.0))
        ktab[:, K_KD128 + h] = g ** (127.0 - i) * HD ** -0.5
        ktab[:, K_KD64 + h] = np.where(i < 64, g ** (63.0 - i).clip(0), 0.0) * HD ** -0.5
    ktab[:, K_IDF:K_IDF + 128] = np.eye(128)
    ktab[:, K_ONES:K_ONES + 128] = 1.0 / C
    half = HD // 2
    inv_freq = 10000.0 ** (-(np.arange(half, dtype=np.float64)) / half)
    pos = np.arange(SEQ, dtype=np.float64)
    ang = inv_freq[:, None].astype(np.float32).astype(np.float64) * pos[None, :]
    ang = ang.astype(np.float32).astype(np.float64)
    cos = np.cos(ang)
    sin = np.sin(ang)
    cst = np.zeros((2, 128, SEQ), np.float64)
    cst[0, :half] = cos
    cst[0, half:] = cos
    cst[1, :half] = sin
    cst[1, half:] = -sin
    cbf = np.zeros((128, 256), np.float32)
    cbf[:, 0:128] = np.eye(128)
    p = np.arange(128)
    cbf[(p + 64) % 128, 128 + p] = 1.0
    return ktab.astype(np.float32), cst.astype(np.float32), cbf


_ORDER = [4, 0, 5, 1, 6, 2, 7, 3] + list(range(12, 20)) + list(range(20, 24)) + list(range(8, 12)) + list(range(24, 28))

_CACHE = {}


def kernel(x_prompt, x_sample, cache_conv, state_ret, norm_pre, w_in, conv_w, conv_b, conv_ln_w, conv_ln_b,
           w_pw, ret_gn_w, w_out, norm_post):
    f = np.float32
    x_prompt = np.asarray(x_prompt, f)
    x_sample = np.asarray(x_sample, f)
    cache_conv = np.asarray(cache_conv, f)
    state_ret = np.asarray(state_ret, f)
    w_in = np.asarray(w_in, f)
    w_pw = np.asarray(w_pw, f)
    w_out = np.asarray(w_out, f)
    wall = np.empty((2, NSLAB, 128, SLABW), f)
    for l in range(2):
        win_s = w_in[l].reshape(16, 128, 28, 256).transpose(2, 1, 0, 3).reshape(28, 128, SLABW)
        wall[l, 0:28] = win_s[_ORDER]
        wall[l, 28:30] = w_pw[l].reshape(8, 128, 2, 512).transpose(2, 1, 0, 3).reshape(2, 128, SLABW)
        wall[l, 30:38] = w_out[l].reshape(16, 128, 8, 256).transpose(2, 1, 0, 3).reshape(8, 128, SLABW)
    pvec = np.zeros((2, 128, NPV), f)
    for l in range(2):
        pvec[l, :, P_NPRE:P_NPRE + 16] = np.asarray(norm_pre, f)[l].reshape(16, 128).T
        pvec[l, :, P_CB:P_CB + 8] = np.asarray(conv_b, f)[l].reshape(8, 128).T
        pvec[l, :, P_LNW:P_LNW + 8] = np.asarray(conv_ln_w, f)[l].reshape(8, 128).T
        pvec[l, :, P_LNB:P_LNB + 8] = np.asarray(conv_ln_b, f)[l].reshape(8, 128).T
        pvec[l, :, P_GNW:P_GNW + 8] = np.asarray(ret_gn_w, f)[l].reshape(8, 128).T
        pvec[l, :, P_CW:] = np.asarray(conv_w, f)[l].reshape(CW, 8, 128).transpose(2, 1, 0).reshape(128, 8 * CW)
    wpost = np.ascontiguousarray(np.broadcast_to(np.asarray(norm_post, f)[:, None, :], (2, 128, D)))
    ktab, cst, cbf = _const_tables()

    if "nc" not in _CACHE:
        _CACHE["nc"] = build_program()[0]
    nc = _CACHE["nc"]
    in_maps = []
    for c in range(8):
        bs = slice(2 * c, 2 * c + 2)
        ccT = np.ascontiguousarray(cache_conv[:, bs].reshape(2, 2, CS, 8, 128).transpose(0, 1, 4, 3, 2))
        in_maps.append({
            "xp": np.ascontiguousarray(x_prompt[c]),
            "xs": np.ascontiguousarray(x_sample[bs].reshape(128, D)),
            "wall": wall, "pvec": pvec, "wpost": wpost, "ccT": ccT,
            "sret": np.ascontiguousarray(state_ret[:, bs]),
            "ktab": ktab, "cstab": cst, "cbf": cbf,
        })
    res = run_bass_kernel_spmd(nc, in_maps, core_ids=list(range(8)))
    rs = res.results
    y_prompt = np.stack([rs[c]["yp"] for c in range(8)], 0)
    y_sample = np.concatenate([rs[c]["ys"].reshape(2, DS, D) for c in range(8)], 0)
    ncp = np.stack([rs[c]["ncp"] for c in range(8)], 1)
    nrp = np.stack([rs[c]["nrp"] for c in range(8)], 1)
    ncs = np.concatenate([rs[c]["ncs"] for c in range(8)], 1)
    nrs = np.concatenate([rs[c]["nrs"] for c in range(8)], 1)
    return (y_prompt.astype(f), y_sample.astype(f), ncp.astype(f), nrp.astype(f), ncs.astype(f), nrs.astype(f))
```

```python
import numpy as np
from contextlib import ExitStack
import concourse.bass as bass
import concourse.mybir as mybir
from concourse.bass_utils import run_bass_kernel_spmd

F32 = mybir.dt.float32
BF16 = mybir.dt.bfloat16
I32 = mybir.dt.int32
AF = mybir.ActivationFunctionType
ALU = mybir.AluOpType
AX = mybir.AxisListType

D = 2048
C = 1024
H = 8
HD = 128
CW = 31
CS = 30
SEQ = 4096
DS = 64
PAST = 2048
EPS = 1e-6
NT = 256
NSLAB = 38
SLABW = 4096
NSLOT = 3
UW = CS + NT
NG = 4
P_NPRE = 0
P_CB = 16
P_LNW = 24
P_LNB = 32
P_GNW = 40
P_CW = 48
NPV = 48 + 8 * CW
K_MASK = 0
K_ODEC = 1024
K_ODEC2 = 1032
K_KD128 = 1040
K_KD64 = 1048
K_IDF = 1056
K_ONES = 1184
NK = 1312
GAM = [1.0 - 2.0 ** (-5.0 - h) for h in range(H)]


PSUM_KEYS = {"G0", "G1", "O0", "O1", "T0", "T1", "SU0", "SU1"}


class Sched:
    ENGS = ["pe", "dve", "act", "pool", "sp"]

    def __init__(self):
        self.ops = []
        self.lastw = {}
        self.readers = {}

    def add(self, eng, fn, reads=(), writes=(), dma_sem=None, ndma=0):
        if getattr(self, "cap", None) is not None:
            self.cap.append((eng, fn, list(reads), list(writes), dma_sem, ndma))
            return -1
        idx = len(self.ops)
        deps = {}
        for k in reads:
            w = self.lastw.get(k)
            if w is not None:
                deps[w] = True
            if k in PSUM_KEYS:
                for r in self.readers.get(k, ()):
                    if self.ops[r]["eng"] != eng:
                        deps.setdefault(r, False)
        for k in writes:
            w = self.lastw.get(k)
            if w is not None:
                deps.setdefault(w, False)
            for r in self.readers.get(k, ()):
                deps.setdefault(r, False)
        for k in writes:
            self.lastw[k] = idx
            self.readers[k] = []
        for k in reads:
            if k not in writes:
                self.readers.setdefault(k, []).append(idx)
        self.ops.append(dict(eng=eng, fn=fn, deps=deps, dma_sem=dma_sem, ndma=ndma, signal=False))
        return idx

    def replay(self, lst, n):
        for _ in range(min(n, len(lst))):
            self.add(*lst.pop(0))

    def emit(self, nc, block, es):
        ops = self.ops
        for op in ops:
            for d in list(op["deps"].keys()):
                raw = op["deps"][d]
                dop = ops[d]
                same = dop["eng"] == op["eng"]
                is_dma = dop["dma_sem"] is not None or op["dma_sem"] is not None
                if same and not is_dma and op["eng"] == "pe":
                    del op["deps"][d]
                    continue
                dop["signal"] = True
        cnt = {}
        semnames = set()
        for op in ops:
            if op["dma_sem"] is not None:
                s = "d_" + op["dma_sem"]
                cnt[s] = cnt.get(s, 0) + 16 * op["ndma"]
                op["sem"], op["count"] = s, cnt[s]
                semnames.add(s)
            elif op["signal"]:
                s = "e_" + op["eng"]
                cnt[s] = cnt.get(s, 0) + 1
                op["sem"], op["count"] = s, cnt[s]
                semnames.add(s)
        semh = {s: es.enter_context(nc.semaphore(s)) for s in sorted(semnames)}
        self.nsem = len(semh)

        def run(eng, e):
            waited = {}
            for op in ops:
                if op["eng"] != eng:
                    continue
                need = {}
                for d in op["deps"]:
                    dop = ops[d]
                    s = dop["sem"]
                    if dop["count"] > need.get(s, 0):
                        need[s] = dop["count"]
                for s, c in need.items():
                    if waited.get(s, 0) >= c:
                        continue
                    e.wait_ge(semh[s], c)
                    waited[s] = c
                if op["fn"] is None:
                    continue
                r = op["fn"](e)
                if op["dma_sem"] is not None:
                    assert len(r) == op["ndma"], (len(r), op["ndma"])
                    for ins in r:
                        ins.then_inc(semh[op["sem"]], 16)
                elif op["signal"]:
                    r.then_inc(semh[op["sem"]], 1)

        @block.tensor
        def _(e):
            run("pe", e)

        @block.vector
        def _(e):
            run("dve", e)

        @block.scalar
        def _(e):
            run("act", e)

        @block.gpsimd
        def _(e):
            run("pool", e)

        @block.sync
        def _(e):
            run("sp", e)


class Tile:
    def __init__(self, kind, idx=0):
        self.kind = kind
        self.idx = idx
        if kind == "p":
            self.ntok = NT
            self.nsub = 2
            self.nseg = 1
            self.Ls = NT
            self.sc = [(0, 128, 0), (128, 128, 0)]
            self.L = 128
            self.pos0 = idx * NT
            self.need_u32 = idx == SEQ // NT - 1
            self.last = idx == SEQ // NT - 1
        else:
            self.ntok = 128
            self.nsub = 1
            self.nseg = 2
            self.Ls = DS
            self.sc = [(0, DS, 1), (DS, DS, 0)]
            self.L = DS
            self.pos0 = PAST
            self.need_u32 = True
            self.last = True


def build_program(cfg=None):
    cfg = cfg or {}
    nc = bass.Bass("TRN2", target_bir_lowering=False)

    def din(name, shape, dt=F32):
        return nc.dram_tensor(name, list(shape), dt, kind="ExternalInput").ap()

    def dout(name, shape):
        return nc.dram_tensor(name, list(shape), F32, kind="ExternalOutput").ap()

    xp = din("xp", [SEQ, D])
    xs = din("xs", [128, D])
    wall = din("wall", [2, NSLAB, 128, SLABW])
    pvec = din("pvec", [2, 128, NPV])
    wpost = din("wpost", [2, 128, D])
    ccT = din("ccT", [2, 2, 128, 8, CS])
    sret = din("sret", [2, 2, H, HD, HD])
    ktab = din("ktab", [128, NK])
    cstab = din("cstab", [2, 128, SEQ])
    cbf = din("cbf", [128, 256])
    wb = nc.dram_tensor("wb", [2, NSLAB, 128, SLABW], BF16, kind="Internal").ap()
    yp = dout("yp", [SEQ, D])
    ys = dout("ys", [128, D])
    ncp = dout("ncp", [2, CS, C])
    nrp = dout("nrp", [2, H, HD, HD])
    ncs = dout("ncs", [2, 2, CS, C])
    nrs = dout("nrs", [2, 2, H, HD, HD])

    S = Sched()
    with ExitStack() as es:
        def sb(name, shape, dt):
            return es.enter_context(nc.sbuf_tensor(name, list(shape), dt))

        def ps(name, shape, dt):
            return es.enter_context(nc.psum_tensor(name, list(shape), dt))

        xbuf = [sb("xbuf0", [128, 2, D], F32), sb("xbuf1", [128, 2, D], F32)]
        htm = sb("htm", [128, 2, D], BF16)
        hT = sb("hT", [128, 16, NT], BF16)
        ymix = sb("ymix", [128, 16, NT], BF16)
        ubuf = [sb(f"ubuf{l}", [128, 8, UW], BF16) for l in range(2)]
        u32 = [sb(f"u32_{g}", [128, 8, 32], F32) for g in range(2)]
        ucT = sb("ucT", [32, C], F32)
        sgc = sb("sgc", [128, 8, NT], BF16)
        sgr = sb("sgr", [128, 8, NT], BF16)
        qT = sb("qT", [128, 8, NT], BF16)
        kT = sb("kT", [128, 8, NT], BF16)
        vtm = sb("vtm", [128, 2, C], BF16)
        cf32 = sb("cf32", [128, 8, NT], F32)
        tb = [sb(f"tb{s}", [128, D], F32) for s in range(2)]
        th = [sb(f"th{i}", [128, NT], F32) for i in range(2)]
        csq = [sb(f"csq{i}", [128, NT], F32) for i in range(2)]
        cn = [sb(f"cn{i}", [128, NT], F32) for i in range(2)]
        rotA = [sb(f"rotA{i}", [128, NT], BF16) for i in range(2)]
        rotB = [sb(f"rotB{i}", [128, NT], BF16) for i in range(2)]
        lmu = sb("lmu", [128, NT], F32)
        lvar = sb("lvar", [128, NT], F32)
        lrs = sb("lrs", [128, NT], F32)
        ltmp = sb("ltmp", [128, NT], F32)
        lnmr = sb("lnmr", [128, NT], F32)
        actc = sb("actc", [128, 8, NT], BF16)
        Sm = sb("Sm", [128, 8, 128], BF16)
        kdtm = sb("kdtm", [128, 8, 128], BF16)
        ontm = sb("ontm", [128, 8, 128], BF16)
        R = [[sb(f"R{l}_0", [128, 8, 128], F32),
              xbuf[1][:, 1, l * 1024:(l + 1) * 1024].rearrange("p (h e) -> p h e", h=8)] for l in range(2)]
        Rbf1 = sb("Rbf", [128, 8, 128], BF16)
        Rbf = [Rbf1, Rbf1]
        wslot = [sb(f"wslot{i}", [128, SLABW], BF16) for i in range(NSLOT)]
        diag = [sb(f"diag{i}", [128, CW, 128], BF16) for i in range(2)]
        cosb = sb("cosb", [128, NT], F32)
        sinb = sb("sinb", [128, NT], F32)
        kt = sb("kt", [128, NK], F32)
        cb16 = sb("cb16", [128, 256], BF16)
        pv = [sb(f"pv{l}", [128, NPV], F32) for l in range(2)]
        wpb1 = sb("wpb", [128, D], F32)
        wpb = [wpb1, wpb1]
        st = sb("st", [128, 64], F32)

        gbank = [ps(f"gb{i}", [128, 512], F32) for i in range(2)]
        TB32 = [ps("tbk0", [128, 512], F32), ps("tbk1", [128, 512], F32)]
        TB = [t[:, 0:256].bitcast(BF16) for t in TB32]
        SU = [ps(f"su{i}", [128, 512], F32) for i in range(2)]
        OB = [ps(f"ob{i}", [128, 512], F32) for i in range(2)]
        G = [gbank[0][:, 0:256], gbank[1][:, 0:256], OB[0][:, 0:256], OB[1][:, 0:256],
             TB32[0][:, 0:256], TB32[1][:, 0:256]]
        GKEY = ["G0", "G1", "O0", "O1", "T0", "T1"]
        ident = cb16[:, 0:128]
        perm = cb16[:, 128:256]
        identf = kt[:, K_IDF:K_IDF + 128]
        onesm = kt[:, K_ONES:K_ONES + 128]
        maskT = kt[:, K_MASK:K_MASK + 1024].rearrange("p (h i) -> p h i", h=8)

        state = dict(g=0, slab=0, rot=0, pool=[0, 1, 2, 3])
        deferred = []

        def run_deferred():
            while deferred:
                deferred.pop(0)()

        def nextg():
            pool = state["pool"]
            i = pool[state["g"] % len(pool)]
            state["g"] += 1
            return i

        ST_SSA, ST_VA, ST_RA, ST_TA = 0, 2, 4, 6
        ST_S1, ST_S2, ST_MEAN, ST_MSQ, ST_VAR, ST_RS, ST_T, ST_A, ST_NB = 8, 16, 24, 32, 40, 48, 56, 8, 16
        st2 = sb("st2", [128, 64], F32)
        ST2_SSQ, ST2_SS, ST2_V, ST2_R, ST2_T = 0, 16, 18, 20, 22
        ST2_A, ST2_NB = 32, 40

        def rsqrt(v, y, t, kv, ky, ktmp, iters=3):
            S.add("dve", lambda e: e.tensor_single_scalar(out=y.bitcast(I32), in_=v.bitcast(I32), scalar=1,
                                                          op=ALU.arith_shift_right), reads=[kv], writes=[ky])
            S.add("dve", lambda e: e.tensor_scalar(out=y.bitcast(I32), in0=y.bitcast(I32), scalar1=-1,
                                                   scalar2=1597463007, op0=ALU.mult, op1=ALU.add),
                  reads=[ky], writes=[ky])
            for _ in range(iters):
                S.add("dve", lambda e: e.scalar_tensor_tensor(out=t, in0=y, scalar=-0.5, in1=y, op0=ALU.mult,
                                                              op1=ALU.mult), reads=[ky], writes=[ktmp])
                S.add("dve", lambda e: e.tensor_tensor(out=t, in0=t, in1=v, op=ALU.mult), reads=[ktmp, kv],
                      writes=[ktmp])
                S.add("dve", lambda e: e.scalar_tensor_tensor(out=y, in0=t, scalar=1.5, in1=y, op0=ALU.add,
                                                              op1=ALU.mult), reads=[ktmp, ky], writes=[ky])

        S.add("pool", lambda e: [e.dma_start(out=cb16[:], in_=cbf)], writes=["cb16"], dma_sem="c0", ndma=1)
        S.add("sp", lambda e: [e.dma_start(out=kt[:], in_=ktab)], writes=["kt"], dma_sem="c1", ndma=1)
        S.add("sp", lambda e: [e.dma_start(out=pv[0][:], in_=pvec[0]), e.dma_start(out=pv[1][:], in_=pvec[1])],
              writes=["pv"], dma_sem="c2", ndma=2)
        first_use = set()

        def load_slab(l, i):
            k = state["slab"] % NSLOT
            state["slab"] += 1
            if (l, i) not in first_use:
                first_use.add((l, i))
                S.add("pool", lambda e: [e.dma_start(out=wslot[k][:], in_=wall[l, i])],
                      reads=(["kt", "pv", "cb16", "x0_0", "x0_1", "x1_0"] if len(first_use) == 1 else []),
                      writes=[f"ws{k}"], dma_sem=f"wsp{k}", ndma=1)
                S.add("sp", lambda e: [e.dma_start(out=wb[l, i], in_=wslot[k][:])], reads=[f"ws{k}"],
                      writes=[f"wb{l}_{i}"], dma_sem=f"wbk{k}", ndma=1)
            else:
                S.add("sp", lambda e: [e.dma_start(out=wslot[k][:], in_=wb[l, i])], reads=[f"wb{l}_{i}"],
                      writes=[f"ws{k}"], dma_sem=f"ws{k}", ndma=1)
            return k

        def stage_a(T, l):
            nsub = T.nsub
            P = pv[l]
            X = xbuf[T.par]
            xk = [f"x{T.par}_{s}" for s in range(2)]
            for s in range(nsub):
                S.add("act", lambda e, s=s: e.activation(out=htm[:, s, :], in_=X[:, s, :], func=AF.Square,
                                                         accum_out=st[:, ST_SSA + s:ST_SSA + s + 1]),
                      reads=[xk[s]], writes=[f"htm{s}", f"ssA{s}"])
            va = st[:, ST_VA:ST_VA + nsub]
            ra = st[:, ST_RA:ST_RA + nsub]
            ta = st[:, ST_TA:ST_TA + nsub]
            S.add("dve", lambda e: e.tensor_scalar(out=va, in0=st[:, ST_SSA:ST_SSA + nsub], scalar1=1.0 / D,
                                                   scalar2=EPS, op0=ALU.mult, op1=ALU.add),
                  reads=[f"ssA{s}" for s in range(nsub)], writes=["vA"])
            rsqrt(va, ra, ta, "vA", "rA", "tA", iters=2)
            for s in range(nsub):
                S.add("dve", lambda e, s=s: e.tensor_scalar(out=htm[:, s, :], in0=X[:, s, :],
                                                            scalar1=st[:, ST_RA + s:ST_RA + s + 1], scalar2=None,
                                                            op0=ALU.mult),
                      reads=[xk[s], "rA"], writes=[f"htm{s}"])

        def stage_a2(T, l):
            nsub = T.nsub
            P = pv[l]
            for s in range(nsub):
                for g in range(4):
                    hf = g % 2
                    tv = TB[hf][:, :]

                    def tr(e, s=s, g=g, tv=tv):
                        r = None
                        for j in range(4):
                            kc = 4 * g + j
                            r = e.transpose(tv[:, j * 128:(j + 1) * 128], htm[:, s, kc * 128:(kc + 1) * 128], ident)
                        return r
                    S.add("pe", tr, reads=[f"htm{s}", "cb16"], writes=[f"T{hf}"])
                    S.add("dve", lambda e, s=s, g=g, tv=tv: e.tensor_tensor(
                        out=hT[:, 4 * g:4 * g + 4, s * 128:(s + 1) * 128],
                        in0=tv.rearrange("p (a b) -> p a b", a=4),
                        in1=P[:, P_NPRE + 4 * g:P_NPRE + 4 * g + 4].unsqueeze(2).to_broadcast([128, 4, 128]),
                        op=ALU.mult), reads=[f"T{hf}", "pv"], writes=[f"hT{s}{g}"])


        def tile_layer(T, l, skip_a=False, hook1=None, hook2=None):
            N = T.ntok
            nsub = T.nsub
            nseg = T.nseg
            Ls = T.Ls
            P = pv[l]
            nsc = len(T.sc)
            state["pool"] = [0, 1, 2, 3]
            X = xbuf[T.par]
            xk = [f"x{T.par}_{s}" for s in range(2)]
            HTK = [f"hT{s}{g}" for s in range(nsub) for g in range(4)]
            UBK = [f"ub{l}_{c}" for c in range(8)]
            jk = [(rotA[0], "rA0"), (rotB[0], "rB0"), (rotA[1], "rA1"), (rotB[1], "rB1")]

            def nextjunk():
                state["jk"] = state.get("jk", 0) + 1
                return jk[state["jk"] % 4]

            if not skip_a:
                stage_a(T, l)
                stage_a2(T, l)

            def fm_group(k, cc, gi):
                wv = wslot[k][:].rearrange("p (kc c) -> p kc c", kc=16)

                def f(e):
                    r = None
                    for kc in range(16):
                        r = e.matmul(G[gi][:, :N], lhsT=wv[:, kc, cc * 128:(cc + 1) * 128], rhs=hT[:, kc, :N],
                                     start=(kc == 0), stop=(kc == 15))
                    return r
                S.add("pe", f, reads=[f"ws{k}"] + HTK, writes=[GKEY[gi]])

            pend = []

            def flush_rot(keep):
                while len(pend) > keep:
                    (which, h, r) = pend.pop(0)
                    gi = nextg()

                    def f(e, r=r, gi=gi):
                        e.matmul(G[gi][:, :N], lhsT=ident, rhs=rotA[r][:, :N], start=True, stop=False)
                        return e.matmul(G[gi][:, :N], lhsT=perm, rhs=rotB[r][:, :N], start=False, stop=True)
                    S.add("pe", f, reads=[f"rA{r}", f"rB{r}", "cb16"], writes=[GKEY[gi]])
                    dst = qT if which == "q" else kT
                    S.add("act", lambda e, gi=gi, dst=dst, h=h: e.activation(out=dst[:, h, :N], in_=G[gi][:, :N],
                                                                             func=AF.Copy),
                          reads=[GKEY[gi]], writes=[f"{which}{h}"])

            def seg_view(ap2d, lo):
                return ap2d[:, 0:nseg * (CS + Ls)].rearrange("p (g w) -> p g w", g=nseg)[:, :, lo:lo + Ls]

            def slab_gate_a(i):
                k = load_slab(l, i)
                is_gate = (i % 2 == 0)
                for cc in range(2):
                    c = 2 * (i // 2) + cc
                    gi = nextg()
                    fm_group(k, cc, gi)
                    if is_gate:
                        S.add("act", lambda e, gi=gi, c=c: e.activation(out=th[c % 2][:, :N], in_=G[gi][:, :N],
                                                                        func=AF.Tanh, scale=0.5),
                              reads=[GKEY[gi]], writes=[f"th{c % 2}"])
                        S.add("dve", lambda e, c=c: e.tensor_scalar(out=th[c % 2][:, :N], in0=th[c % 2][:, :N],
                                                                    scalar1=0.5, scalar2=0.5, op0=ALU.mult,
                                                                    op1=ALU.add),
                              reads=[f"th{c % 2}"], writes=[f"th{c % 2}"])
                    else:
                        S.add("dve", lambda e, gi=gi, c=c: e.tensor_tensor(
                            out=seg_view(ubuf[l][:, c, :], CS),
                            in0=G[gi][:, :N].rearrange("p (g w) -> p g w", g=nseg),
                            in1=th[c % 2][:, :N].rearrange("p (g w) -> p g w", g=nseg), op=ALU.mult),
                            reads=[GKEY[gi], f"th{c % 2}"], writes=[UBK[c]])
                        if T.need_u32:
                            for g in range(nseg):
                                hi = (g + 1) * Ls
                                S.add("dve", lambda e, gi=gi, c=c, g=g, hi=hi: e.tensor_tensor(
                                    out=u32[g][:, c, :], in0=G[gi][:, hi - 32:hi], in1=th[c % 2][:, hi - 32:hi],
                                    op=ALU.mult), reads=[GKEY[gi], f"th{c % 2}"], writes=[f"u32_{g}_{c}"])

            def slab_qk(i):
                k = load_slab(l, i)
                which = "q" if i < 12 else "k"
                for cc in range(2):
                    h = 2 * ((i - 8) % 4) + cc
                    gi = nextg()
                    fm_group(k, cc, gi)
                    r = state["rot"] % 2
                    state["rot"] += 1
                    S.add("dve", lambda e, gi=gi, r=r: e.tensor_tensor(out=rotA[r][:, :N], in0=G[gi][:, :N],
                                                                       in1=cosb[:, :N], op=ALU.mult),
                          reads=[GKEY[gi], "cos"], writes=[f"rA{r}"])
                    S.add("dve", lambda e, gi=gi, r=r: e.tensor_tensor(out=rotB[r][:, :N], in0=G[gi][:, :N],
                                                                       in1=sinb[:, :N], op=ALU.mult),
                          reads=[GKEY[gi], "cos"], writes=[f"rB{r}"])
                    pend.append((which, h, r))
                    flush_rot(1)

            def slab_v(i):
                k = load_slab(l, i)
                j = i - 16
                wv = wslot[k][:].rearrange("p (kc c) -> p kc c", kc=16)
                for sci, (c0, L, slot) in enumerate(T.sc):
                    gi = nextg()

                    def f(e, gi=gi, c0=c0, L=L, wv=wv):
                        r = None
                        for kc in range(16):
                            r = e.matmul(G[gi][:L, :], lhsT=hT[:, kc, c0:c0 + L], rhs=wv[:, kc, :],
                                         start=(kc == 0), stop=(kc == 15))
                        return r
                    S.add("pe", f, reads=[f"ws{k}"] + [f"hT{c0 // 128}{g}" for g in range(4)], writes=[GKEY[gi]])
                    S.add("act", lambda e, gi=gi, L=L, sci=sci, j=j: e.activation(
                        out=vtm[:L, sci, j * 256:(j + 1) * 256], in_=G[gi][:L, :], func=AF.Copy),
                        reads=[GKEY[gi]], writes=[f"v{sci}_{j}"])

            def slab_gate(i):
                k = load_slab(l, i)
                dst, nm = (sgc, "sgc") if i < 24 else (sgr, "sgr")
                for cc in range(2):
                    c = 2 * ((i - 20) % 4) + cc
                    gi = nextg()
                    fm_group(k, cc, gi)
                    S.add("act", lambda e, gi=gi, c=c, dst=dst: e.activation(out=dst[:, c, :N], in_=G[gi][:, :N],
                                                                             func=AF.Silu),
                          reads=[GKEY[gi]], writes=[f"{nm}{c}"])

            VK = [[f"v{sci}_{j}" for j in range(4)] for sci in range(2)]

            def build_diag(c):
                dg = diag[c % 2]
                for eng_, t0_, t1_ in (("pool", 0, 12), ("dve", 12, CW)):
                    S.add(eng_, lambda e, c=c, dg=dg, t0_=t0_, t1_=t1_: e.tensor_tensor(
                        out=dg[:, t0_:t1_, :], in0=ident.unsqueeze(1).to_broadcast([128, t1_ - t0_, 128]),
                        in1=P[:, P_CW + c * CW + t0_:P_CW + c * CW + t1_].unsqueeze(2).to_broadcast([128, t1_ - t0_, 128]),
                        op=ALU.mult), reads=["cb16", "pv"], writes=[f"dg{c % 2}_{eng_}"])

            for i in range(8):
                slab_gate_a(i)
                if i == 3:
                    build_diag(0)
                if i == 5:
                    build_diag(1)

            if T.need_u32:
                for g in range(nseg):
                    for pc in range(4):
                        gi = nextg()

                        def f(e, g=g, pc=pc, gi=gi):
                            e.matmul(G[gi][:32, 0:128], lhsT=u32[g][:, 2 * pc, :], rhs=identf, start=True, stop=True)
                            return e.matmul(G[gi][:32, 128:256], lhsT=u32[g][:, 2 * pc + 1, :], rhs=identf,
                                            start=True, stop=True)
                        S.add("pe", f, reads=[f"u32_{g}_{2 * pc}", f"u32_{g}_{2 * pc + 1}", "kt"], writes=[GKEY[gi]])
                        S.add("act", lambda e, gi=gi, pc=pc: e.activation(out=ucT[:32, pc * 256:(pc + 1) * 256],
                                                                          in_=G[gi][:32, :], func=AF.Copy),
                              reads=[GKEY[gi]], writes=[f"ucT{pc}"])
                    dst = ncp[l] if T.kind == "p" else ncs[l, g]
                    S.add("sp", lambda e, dst=dst: [e.dma_start(out=dst, in_=ucT[2:32, :])],
                          reads=[f"ucT{pc}" for pc in range(4)],
                          writes=[f"o_nc{T.kind}{l}{g}"], dma_sem="onc", ndma=1)

            run_deferred()
            gm, gq = 4, 5

            def stats(c):
                def f(e):
                    e.matmul(G[gm][:, :N], lhsT=onesm, rhs=cf32[:, c, :N], start=(c == 0), stop=(c == 7))
                    return e.matmul(G[gq][:, :N], lhsT=onesm, rhs=csq[c % 2][:, :N], start=(c == 0), stop=(c == 7))
                S.add("pe", f, reads=[f"cf{c}", f"csq{c % 2}", "kt"], writes=[GKEY[gm], GKEY[gq]])

            for c in range(8):
                dg = diag[c % 2]
                gi = nextg()

                def f(e, c=c, dg=dg, gi=gi):
                    r = None
                    for g in range(nseg):
                        uo = g * (CS + Ls)
                        for kk in range(CW):
                            r = e.matmul(G[gi][:, g * Ls:(g + 1) * Ls], lhsT=dg[:, kk, :],
                                         rhs=ubuf[l][:, c, uo + kk:uo + kk + Ls], start=(kk == 0), stop=(kk == CW - 1))
                    return r
                S.add("pe", f, reads=[f"dg{c % 2}_pool", f"dg{c % 2}_dve", UBK[c]], writes=[GKEY[gi]])
                if c + 2 < 8:
                    build_diag(c + 2)
                S.add("act", lambda e, c=c, gi=gi: e.activation(out=cf32[:, c, :N], in_=G[gi][:, :N], func=AF.Identity,
                                                                bias=P[:, P_CB + c:P_CB + c + 1]),
                      reads=[GKEY[gi], "pv"], writes=[f"cf{c}"])
                S.add("act", lambda e, c=c, gi=gi: e.activation(out=csq[c % 2][:, :N], in_=G[gi][:, :N], func=AF.Square,
                                                                bias=P[:, P_CB + c:P_CB + c + 1]),
                      reads=[GKEY[gi], "pv"], writes=[f"csq{c % 2}"])
                if c >= 1:
                    stats(c - 1)
            stats(7)
            S.add("act", lambda e: e.activation(out=lmu[:, :N], in_=G[gm][:, :N], func=AF.Copy), reads=[GKEY[gm]],
                  writes=["lmu"])
            S.add("dve", lambda e: e.tensor_tensor(out=ltmp[:, :N], in0=lmu[:, :N], in1=lmu[:, :N], op=ALU.mult),
                  reads=["lmu"], writes=["ltmp"])
            S.add("dve", lambda e: e.scalar_tensor_tensor(out=lvar[:, :N], in0=G[gq][:, :N], scalar=EPS, in1=ltmp[:, :N],
                                                          op0=ALU.add, op1=ALU.subtract),
                  reads=[GKEY[gq], "ltmp"], writes=["lvar"])
            if T.kind == "p" and not T.last:
                S.add("dve", lambda e: e.tensor_copy(out=ubuf[l][:, :, 0:CS], in_=ubuf[l][:, :, NT:NT + CS]),
                      reads=UBK, writes=UBK)
            rsqrt(lvar[:, :N], lrs[:, :N], ltmp[:, :N], "lvar", "lrs", "ltmp", iters=2)
            S.add("dve", lambda e: e.scalar_tensor_tensor(out=lnmr[:, :N], in0=lmu[:, :N], scalar=-1.0, in1=lrs[:, :N],
                                                          op0=ALU.mult, op1=ALU.mult),
                  reads=["lmu", "lrs"], writes=["lnmr"])

            def ln_norm(c):
                S.add("dve", lambda e, c=c: e.tensor_tensor(out=cn[c % 2][:, :N], in0=cf32[:, c, :N], in1=lrs[:, :N],
                                                            op=ALU.mult), reads=[f"cf{c}", "lrs"], writes=[f"cn{c % 2}"])
                S.add("dve", lambda e, c=c: e.tensor_tensor(out=cn[c % 2][:, :N], in0=cn[c % 2][:, :N], in1=lnmr[:, :N],
                                                            op=ALU.add), reads=[f"cn{c % 2}", "lnmr"],
                      writes=[f"cn{c % 2}"])
                S.add("act", lambda e, c=c: e.activation(out=actc[:, c, :N], in_=cn[c % 2][:, :N], func=AF.Silu,
                                                         bias=P[:, P_LNB + c:P_LNB + c + 1],
                                                         scale=P[:, P_LNW + c:P_LNW + c + 1]),
                      reads=[f"cn{c % 2}", "pv"], writes=[f"ac{c}"])

            lnc = 0
            for i in range(16, 20):
                slab_v(i)
            for i in range(8, 16):
                slab_qk(i)
                if lnc < 8:
                    ln_norm(lnc)
                    lnc += 1
            flush_rot(0)

            def ret_scores(sci):
                c0, L, slot = T.sc[sci]
                for hb in range(2):
                    def f(e, hb=hb):
                        r = None
                        for hh in range(4):
                            h = 4 * hb + hh
                            r = e.matmul(SU[hb][:L, hh * 128:hh * 128 + L], lhsT=kT[:, h, c0:c0 + L],
                                         rhs=qT[:, h, c0:c0 + L], start=True, stop=True)
                        return r
                    S.add("pe", f, reads=[f"k{4 * hb + j}" for j in range(4)] + [f"q{4 * hb + j}" for j in range(4)],
                          writes=[f"SU{hb}"])
                    S.add("dve", lambda e, hb=hb: e.tensor_tensor(
                        out=Sm[:L, 4 * hb:4 * hb + 4, :L],
                        in0=SU[hb][:L, :].rearrange("p (h i) -> p h i", h=4)[:, :, :L],
                        in1=maskT[:L, 4 * hb:4 * hb + 4, :L], op=ALU.mult),
                        reads=[f"SU{hb}", "kt"], writes=[f"Sm{hb}"])

                def tr(e):
                    r = None
                    for h in range(8):
                        r = e.transpose(TB[h // 4][:L, (h % 4) * 128:(h % 4 + 1) * 128], kT[:, h, c0:c0 + L], ident)
                    return r
                S.add("pe", tr, reads=[f"k{h}" for h in range(8)] + ["cb16"], writes=["T0", "T1"])
                kdc = K_KD128 if L == 128 else K_KD64
                for h in range(8):
                    S.add("dve", lambda e, h=h: e.tensor_scalar(
                        out=kdtm[:L, h, :], in0=TB[h // 4][:L, (h % 4) * 128:(h % 4 + 1) * 128],
                        scalar1=kt[:L, kdc + h:kdc + h + 1], scalar2=None, op0=ALU.mult),
                        reads=[f"T{h // 4}", "kt"], writes=[f"kd{h}"])

            def ret_out(sci, chain=None):
                c0, L, slot = T.sc[sci]
                RK = [f"R{l}_{slot}_{h}" for h in range(8)]
                S.add("act", lambda e: e.activation(out=Rbf[l][:], in_=R[l][slot][:], func=AF.Copy),
                      reads=RK, writes=["Rbf"])
                for hb in range(2):
                    def f(e, hb=hb):
                        r = None
                        for hh in range(4):
                            h = 4 * hb + hh
                            e.matmul(OB[hb][:L, hh * 128:(hh + 1) * 128], lhsT=Sm[:L, h, :L],
                                     rhs=vtm[:L, sci, h * 128:(h + 1) * 128], start=True, stop=False)
                            r = e.matmul(OB[hb][:L, hh * 128:(hh + 1) * 128], lhsT=qT[:, h, c0:c0 + L],
                                         rhs=Rbf[l][:, h, :], start=False, stop=True)
                        return r
                    S.add("pe", f, reads=[f"Sm{hb}", "Rbf"] + VK[sci] + [f"q{4 * hb + j}" for j in range(4)],
                          writes=[f"O{hb}"])
                for hb in range(2):
                    def f(e, hb=hb):
                        r = None
                        for hh in range(4):
                            h = 4 * hb + hh
                            r = e.matmul(SU[hb][:, hh * 128:(hh + 1) * 128], lhsT=kdtm[:L, h, :],
                                         rhs=vtm[:L, sci, h * 128:(h + 1) * 128], start=True, stop=True)
                        return r
                    S.add("pe", f, reads=[f"kd{4 * hb + j}" for j in range(4)] + VK[sci], writes=[f"SU{hb}"])
                    for hh in range(4):
                        h = 4 * hb + hh
                        S.add("dve", lambda e, hb=hb, hh=hh, h=h: e.scalar_tensor_tensor(
                            out=R[l][slot][:, h, :], in0=R[l][slot][:, h, :], scalar=float(GAM[h] ** L),
                            in1=SU[hb][:, hh * 128:(hh + 1) * 128], op0=ALU.mult, op1=ALU.add),
                            reads=[RK[h], f"SU{hb}"], writes=[RK[h]])
                for hb in range(2):
                    S.add("dve", lambda e, hb=hb: e.tensor_reduce(
                        out=st[:L, ST_S1 + 4 * hb:ST_S1 + 4 * hb + 4],
                        in_=OB[hb][:L, :].rearrange("p (h e) -> p h e", h=4), axis=AX.X, op=ALU.add),
                        reads=[f"O{hb}"], writes=[f"s1_{hb}"])
                    for hh in range(4):
                        h = 4 * hb + hh
                        jt, jkey = nextjunk()
                        S.add("act", lambda e, hb=hb, hh=hh, h=h, jt=jt: e.activation(
                            out=jt[:L, 0:128], in_=OB[hb][:L, hh * 128:(hh + 1) * 128], func=AF.Square,
                            accum_out=st[:L, ST_S2 + h:ST_S2 + h + 1]), reads=[f"O{hb}"], writes=[f"s2_{h}", jkey])
                S.cap = chain
                mean = st[:L, ST_MEAN:ST_MEAN + 8]
                msq = st[:L, ST_MSQ:ST_MSQ + 8]
                var = st[:L, ST_VAR:ST_VAR + 8]
                rs = st[:L, ST_RS:ST_RS + 8]
                tt = st[:L, ST_T:ST_T + 8]
                aa = st2[:L, ST2_A:ST2_A + 8]
                nb = st2[:L, ST2_NB:ST2_NB + 8]
                S.add("dve", lambda e: e.tensor_scalar(out=mean, in0=st[:L, ST_S1:ST_S1 + 8], scalar1=1.0 / HD,
                                                       scalar2=None, op0=ALU.mult), reads=["s1_0", "s1_1"],
                      writes=["gmean"])
                S.add("dve", lambda e: e.tensor_tensor(out=msq, in0=mean, in1=mean, op=ALU.mult), reads=["gmean"],
                      writes=["gmsq"])
                S.add("dve", lambda e: e.scalar_tensor_tensor(out=var, in0=st[:L, ST_S2:ST_S2 + 8], scalar=1.0 / HD,
                                                              in1=msq, op0=ALU.mult, op1=ALU.subtract),
                      reads=[f"s2_{h}" for h in range(8)] + ["gmsq"], writes=["gvar"])
                S.add("dve", lambda e: e.tensor_tensor(out=var, in0=var, in1=kt[:L, K_ODEC2:K_ODEC2 + 8], op=ALU.mult),
                      reads=["gvar", "kt"], writes=["gvar"])
                S.add("dve", lambda e: e.tensor_scalar(out=var, in0=var, scalar1=EPS, scalar2=None, op0=ALU.add),
                      reads=["gvar"], writes=["gvar"])
                rsqrt(var, rs, tt, "gvar", "grs", "gtt", iters=2)
                S.add("dve", lambda e: e.tensor_tensor(out=aa, in0=rs, in1=kt[:L, K_ODEC:K_ODEC + 8], op=ALU.mult),
                      reads=["grs", "kt"], writes=["ga"])
                S.add("dve", lambda e: e.scalar_tensor_tensor(out=nb, in0=mean, scalar=-1.0, in1=aa, op0=ALU.mult,
                                                              op1=ALU.mult), reads=["gmean", "ga"], writes=["gnb"])
                S.cap = None

            def ret_norm(sci):
                c0, L, slot = T.sc[sci]
                for hb in range(2):
                    for hh in range(4):
                        h = 4 * hb + hh
                        S.add("act", lambda e, hb=hb, hh=hh, h=h: e.activation(
                            out=ontm[:L, h, :], in_=OB[hb][:L, hh * 128:(hh + 1) * 128], func=AF.Identity,
                            bias=st2[:L, ST2_NB + h:ST2_NB + h + 1], scale=st2[:L, ST2_A + h:ST2_A + h + 1]),
                            reads=[f"O{hb}", "ga", "gnb"], writes=[f"on{h}"])

            def ret_fin(sci):
                c0, L, slot = T.sc[sci]

                def tr(e):
                    r = None
                    for h in range(8):
                        r = e.transpose(TB[h // 4][:, (h % 4) * 128:(h % 4) * 128 + L], ontm[:L, h, :], ident[:L, :L])
                    return r
                S.add("pe", tr, reads=[f"on{h}" for h in range(8)] + ["cb16"], writes=["T0", "T1"])
                for h in range(8):
                    S.add("dve", lambda e, h=h: e.scalar_tensor_tensor(
                        out=ymix[:, 8 + h, c0:c0 + L], in0=TB[h // 4][:, (h % 4) * 128:(h % 4) * 128 + L],
                        scalar=P[:, P_GNW + h:P_GNW + h + 1], in1=sgr[:, h, c0:c0 + L], op0=ALU.mult, op1=ALU.mult),
                        reads=[f"T{h // 4}", "pv", f"sgr{h}"], writes=[f"ym{c0 // 128}_{8 + h}_{sci}"])

            state["pool"] = [0, 1]
            ret_scores(0)
            ret_out(0)
            S.add("sp", lambda e: [e.dma_start(out=wpb[l][:], in_=wpost[l])], writes=["wpb"], dma_sem="c3", ndma=1)
            if hook1 is not None:
                hook1()
            state["pool"] = [0, 1, 4, 5]
            for i in range(24, 28):
                slab_gate(i)
            if nsc > 1:
                ret_scores(1)
            slab_gate(20)
            ret_norm(0)
            for i in range(21, 24):
                slab_gate(i)
            ret_fin(0)
            chain = []
            if nsc > 1:
                ret_out(1, chain)

            for j in range(2):
                k = load_slab(l, 28 + j)
                wv = wslot[k][:].rearrange("p (kc c) -> p kc c", kc=8)
                for mm in range(4):
                    m = 4 * j + mm
                    gi = nextg()

                    def f(e, wv=wv, mm=mm, gi=gi):
                        r = None
                        for kc in range(8):
                            r = e.matmul(G[gi][:, :N], lhsT=wv[:, kc, mm * 128:(mm + 1) * 128], rhs=actc[:, kc, :N],
                                         start=(kc == 0), stop=(kc == 7))
                        return r
                    S.add("pe", f, reads=[f"ws{k}"] + [f"ac{c}" for c in range(8)], writes=[GKEY[gi]])
                    S.add("dve", lambda e, m=m, gi=gi: e.tensor_tensor(out=ymix[:, m, :N], in0=G[gi][:, :N],
                                                                       in1=sgc[:, m, :N], op=ALU.mult),
                          reads=[GKEY[gi], f"sgc{m}"], writes=[f"ym{s_}_{m}" for s_ in range(nsub)])
                    S.replay(chain, 3)
            S.replay(chain, len(chain))
            if hook2 is not None:
                hook2()
            if nsc > 1:
                ret_norm(1)
                ret_fin(1)

            state["pool"] = [0, 1, 2, 3]

            def ymk(s):
                ks = [f"ym{s}_{m}" for m in range(8)]
                for sci, (c0, L, slot) in enumerate(T.sc):
                    if c0 // 128 == s:
                        ks += [f"ym{s}_{8 + h}_{sci}" for h in range(8)]
                return ks
            for cg in range(8):
                k = load_slab(l, 30 + cg)
                wv = wslot[k][:].rearrange("p (kc c) -> p kc c", kc=16)
                for s in range(nsub):
                    gi = nextg()

                    def f(e, wv=wv, s=s, gi=gi):
                        r = None
                        for kc in range(16):
                            r = e.matmul(G[gi][:, :], lhsT=ymix[:, kc, s * 128:(s + 1) * 128], rhs=wv[:, kc, :],
                                         start=(kc == 0), stop=(kc == 15))
                        return r
                    S.add("pe", f, reads=[f"ws{k}"] + ymk(s), writes=[GKEY[gi]])
                    S.add("dve", lambda e, s=s, cg=cg, gi=gi: e.tensor_tensor(
                        out=tb[s][:, cg * 256:(cg + 1) * 256], in0=G[gi][:, :], in1=wpb[l][:, cg * 256:(cg + 1) * 256],
                        op=ALU.mult), reads=[GKEY[gi], "wpb"], writes=[f"tb{s}_{cg}"])
                    jt, jkey = nextjunk()
                    S.add("act", lambda e, s=s, cg=cg, gi=gi, jt=jt: e.activation(
                        out=jt[:, :], in_=G[gi][:, :], func=AF.Square,
                        accum_out=st2[:, ST2_SSQ + 8 * s + cg:ST2_SSQ + 8 * s + cg + 1]),
                        reads=[GKEY[gi]], writes=[f"ssq{s}_{cg}", jkey])
            S.add("dve", lambda e: e.tensor_reduce(
                out=st2[:, ST2_SS:ST2_SS + nsub],
                in_=st2[:, ST2_SSQ:ST2_SSQ + 8 * nsub].rearrange("p (s c) -> p s c", s=nsub), axis=AX.X, op=ALU.add),
                reads=[f"ssq{s}_{cg}" for s in range(nsub) for cg in range(8)], writes=["ssE"])
            ve = st2[:, ST2_V:ST2_V + nsub]
            re_ = st2[:, ST2_R:ST2_R + nsub]
            te = st2[:, ST2_T:ST2_T + nsub]
            S.add("dve", lambda e: e.tensor_scalar(out=ve, in0=st2[:, ST2_SS:ST2_SS + nsub], scalar1=1.0 / D,
                                                   scalar2=EPS, op0=ALU.mult, op1=ALU.add), reads=["ssE"], writes=["vE"])
            rsqrt(ve, re_, te, "vE", "rE", "tE", iters=2)
            for s in range(nsub):
                TBK = [f"tb{s}_{cg}" for cg in range(8)]
                if l == 0:
                    S.add("dve", lambda e, s=s: e.scalar_tensor_tensor(
                        out=X[:, s, :], in0=tb[s][:], scalar=st2[:, ST2_R + s:ST2_R + s + 1], in1=X[:, s, :],
                        op0=ALU.mult, op1=ALU.add), reads=TBK + ["rE", xk[s]],
                        writes=[xk[s]] + ([f"R{ll}_1_{h}" for ll in range(2) for h in range(8)]
                                          if (T.kind == "p" and T.idx == 1 and s == 1) else []))
                else:
                    S.add("dve", lambda e, s=s: e.scalar_tensor_tensor(
                        out=tb[s][:], in0=tb[s][:], scalar=st2[:, ST2_R + s:ST2_R + s + 1], in1=X[:, s, :],
                        op0=ALU.mult, op1=ALU.add), reads=TBK + ["rE", xk[s]], writes=TBK)

        ptiles = [Tile("p", i) for i in range(SEQ // NT)]
        stile = Tile("s")
        for T in ptiles:
            T.par = T.idx % 2
        stile.par = 1
        tasks = []
        for k in range(0, len(ptiles), 2):
            A, B = ptiles[k], ptiles[k + 1]
            tasks += [(A, 0), (B, 0), (A, 1), (B, 1)]
        tasks = [(stile, 0), (stile, 1)] + tasks
        if "ntasks" in cfg:
            tasks = tasks[:cfg["ntasks"]]

        def load_x(T):
            X = xbuf[T.par]
            if T.kind == "s":
                S.add("sp", lambda e: [e.dma_start(out=X[:, 0, :], in_=xs)], writes=[f"x{T.par}_0"],
                      dma_sem=f"x{T.par}0", ndma=1)
            else:
                for s in range(T.nsub):
                    r0 = T.idx * NT + s * 128
                    extra = []
                    if T.idx == 1 and s == 1:
                        extra = [f"R{ll}_1_{h}" for ll in range(2) for h in range(8)]
                    S.add("sp", lambda e, s=s, r0=r0: [e.dma_start(out=X[:, s, :], in_=xp[r0:r0 + 128, :])],
                          writes=[f"x{T.par}_{s}"] + extra, dma_sem=f"x{T.par}{s}", ndma=1)

        def load_cos(T):
            if T.kind == "s":
                S.add("sp", lambda e: [e.dma_start(out=cosb[:, 0:DS], in_=cstab[0][:, PAST:PAST + DS]),
                                       e.dma_start(out=cosb[:, DS:2 * DS], in_=cstab[0][:, PAST:PAST + DS]),
                                       e.dma_start(out=sinb[:, 0:DS], in_=cstab[1][:, PAST:PAST + DS]),
                                       e.dma_start(out=sinb[:, DS:2 * DS], in_=cstab[1][:, PAST:PAST + DS])],
                      writes=["cos"], dma_sem="cos", ndma=4)
            else:
                p0 = T.pos0
                S.add("sp", lambda e, p0=p0: [e.dma_start(out=cosb[:, :], in_=cstab[0][:, p0:p0 + NT]),
                                              e.dma_start(out=sinb[:, :], in_=cstab[1][:, p0:p0 + NT])],
                      writes=["cos"], dma_sem="cos", ndma=2)

        load_x(stile)
        load_x(ptiles[0])
        for ll in range(2):
            S.add("sp", lambda e, ll=ll: [e.dma_start(out=R[ll][0][:], in_=sret[ll, 1].rearrange("h d e -> d h e"))],
                  writes=[f"R{ll}_0_{h}" for h in range(8)], dma_sem=f"r{ll}0", ndma=1)
            S.add("sp", lambda e, ll=ll: [e.dma_start(out=R[ll][1], in_=sret[ll, 0].rearrange("h d e -> d h e"))],
                  writes=[f"R{ll}_1_{h}" for h in range(8)] + ["x1_1"], dma_sem=f"r{ll}1", ndma=1)
            uo = [0, CS + DS]
            S.add("pool", lambda e, ll=ll, uo=uo: [e.dma_start(out=ubuf[ll][:, :, uo[b]:uo[b] + CS], in_=ccT[ll, b])
                                                   for b in range(2)],
                  writes=[f"ub{ll}_{c}" for c in range(8)], dma_sem=f"cc{ll}", ndma=2)
        hoisted = False
        for ti, (T, l) in enumerate(tasks):
            N = T.ntok
            load_cos(T)
            if T.kind == "p" and T.idx == 0 and l == 0:
                run_deferred()
                for ll in range(2):
                    S.add("dve", lambda e, ll=ll: e.memset(R[ll][0][:], 0.0), writes=[f"R{ll}_0_{h}" for h in range(8)])
                    S.add("dve", lambda e, ll=ll: e.memset(ubuf[ll][:, :, 0:CS], 0.0),
                          writes=[f"ub{ll}_{c}" for c in range(8)])
            nxt = tasks[ti + 1] if ti + 1 < len(tasks) else None
            hook1 = hook2 = None
            can_hoist = nxt is not None and nxt[0] is not T and cfg.get("hoist", True)
            if can_hoist:
                hook1 = (lambda nT=nxt[0], nl=nxt[1]: stage_a(nT, nl))
                hook2 = (lambda nT=nxt[0], nl=nxt[1]: stage_a2(nT, nl))
            tile_layer(T, l, skip_a=hoisted, hook1=hook1, hook2=hook2)
            hoisted = can_hoist
            def post(T=T, l=l, ti=ti):
                if l != 1:
                    return
                for s in range(T.nsub):
                    if T.kind == "s":
                        dst = ys
                    else:
                        r0 = T.idx * NT + s * 128
                        dst = yp[r0:r0 + 128, :]
                    S.add("sp", lambda e, s=s, dst=dst: [e.dma_start(out=dst, in_=tb[s][:])],
                          reads=[f"tb{s}_{cg}" for cg in range(8)],
                          writes=[f"o_y{ti}_{s}"], dma_sem=f"oy{s}", ndma=1)
                if T.last:
                    for ll in range(2):
                        if T.kind == "s":
                            for b in range(2):
                                S.add("sp", lambda e, ll=ll, b=b: [e.dma_start(out=nrs[ll, b].rearrange("h d e -> d h e"),
                                                                               in_=R[ll][1 - b][:])],
                                      reads=[f"R{ll}_{1 - b}_{h}" for h in range(8)], writes=[f"o_rs{ll}{b}"],
                                      dma_sem=f"orr{ll}{1 - b}", ndma=1)
                        else:
                            S.add("sp", lambda e, ll=ll: [e.dma_start(out=nrp[ll].rearrange("h d e -> d h e"),
                                                                      in_=R[ll][0][:])],
                                  reads=[f"R{ll}_0_{h}" for h in range(8)], writes=[f"o_rp{ll}"], dma_sem=f"orr{ll}0", ndma=1)
                if T.kind == "p":
                    if T.idx + 2 < len(ptiles):
                        load_x(ptiles[T.idx + 2])
                else:
                    load_x(ptiles[1])
            deferred.append(post)
        run_deferred()
        S.add("sp", None, reads=[], writes=[])
        fin = S.ops[-1]
        for i, op in enumerate(S.ops[:-1]):
            if op["dma_sem"] is not None and (op["dma_sem"].startswith("oy") or op["dma_sem"].startswith("orr") or op["dma_sem"] == "onc"):
                fin["deps"][i] = True

        block = es.enter_context(nc.Block())
        S.emit(nc, block, es)
    return nc, S


def _const_tables():
    ktab = np.zeros((128, NK), np.float64)
    i = np.arange(128)
    for h in range(H):
        g = GAM[h]
        ii = i[None, :]
        jj = i[:, None]
        same = (ii // 64) == (jj // 64)
        earlier = (jj // 64) < (ii // 64)
        dm = np.where(same, g ** np.abs(ii - jj), np.where(earlier, g ** (ii - jj).clip(0), 0.0))
        m = dm * g ** (-(ii + 1.0)) * HD ** -0.5
        ktab[:, K_MASK + h * 128:K_MASK + (h + 1) * 128] = m
        ktab[:, K_ODEC + h] = g ** (i + 1.0)
        ktab[:, K_ODEC2 + h] = g ** (2.0 * (i + 1.0))
        ktab[:, K_KD128 + h] = g ** (127.0 - i) * HD ** -0.5
        ktab[:, K_KD64 + h] = np.where(i < 64, g ** (63.0 - i).clip(0), 0.0) * HD ** -0.5
    ktab[:, K_IDF:K_IDF + 128] = np.eye(128)
    ktab[:, K_ONES:K_ONES + 128] = 1.0 / C
    half = HD // 2
    inv_freq = 10000.0 ** (-(np.arange(half, dtype=np.float64)) / half)
    pos = np.arange(SEQ, dtype=np.float64)
    ang = inv_freq[:, None].astype(np.float32).astype(np.float64) * pos[None, :]
    ang = ang.astype(np.float32).astype(np.float64)
    cos = np.cos(ang)
    sin = np.sin(ang)
    cst = np.zeros((2, 128, SEQ), np.float64)
    cst[0, :half] = cos
    cst[0, half:] = cos
    cst[1, :half] = sin
    cst[1, half:] = -sin
    cbf = np.zeros((128, 256), np.float32)
    cbf[:, 0:128] = np.eye(128)
    p = np.arange(128)
    cbf[(p + 64) % 128, 128 + p] = 1.0
    return ktab.astype(np.float32), cst.astype(np.float32), cbf


_ORDER = [4, 0, 5, 1, 6, 2, 7, 3] + list(range(12, 20)) + list(range(20, 24)) + list(range(8, 12)) + list(range(24, 28))

_CACHE = {}


def kernel(x_prompt, x_sample, cache_conv, state_ret, norm_pre, w_in, conv_w, conv_b, conv_ln_w, conv_ln_b,
           w_pw, ret_gn_w, w_out, norm_post):
    f = np.float32
    x_prompt = np.asarray(x_prompt, f)
    x_sample = np.asarray(x_sample, f)
    cache_conv = np.asarray(cache_conv, f)
    state_ret = np.asarray(state_ret, f)
    w_in = np.asarray(w_in, f)
    w_pw = np.asarray(w_pw, f)
    w_out = np.asarray(w_out, f)
    wall = np.empty((2, NSLAB, 128, SLABW), f)
    for l in range(2):
        win_s = w_in[l].reshape(16, 128, 28, 256).transpose(2, 1, 0, 3).reshape(28, 128, SLABW)
        wall[l, 0:28] = win_s[_ORDER]
        wall[l, 28:30] = w_pw[l].reshape(8, 128, 2, 512).transpose(2, 1, 0, 3).reshape(2, 128, SLABW)
        wall[l, 30:38] = w_out[l].reshape(16, 128, 8, 256).transpose(2, 1, 0, 3).reshape(8, 128, SLABW)
    pvec = np.zeros((2, 128, NPV), f)
    for l in range(2):
        pvec[l, :, P_NPRE:P_NPRE + 16] = np.asarray(norm_pre, f)[l].reshape(16, 128).T
        pvec[l, :, P_CB:P_CB + 8] = np.asarray(conv_b, f)[l].reshape(8, 128).T
        pvec[l, :, P_LNW:P_LNW + 8] = np.asarray(conv_ln_w, f)[l].reshape(8, 128).T
        pvec[l, :, P_LNB:P_LNB + 8] = np.asarray(conv_ln_b, f)[l].reshape(8, 128).T
        pvec[l, :, P_GNW:P_GNW + 8] = np.asarray(ret_gn_w, f)[l].reshape(8, 128).T
        pvec[l, :, P_CW:] = np.asarray(conv_w, f)[l].reshape(CW, 8, 128).transpose(2, 1, 0).reshape(128, 8 * CW)
    wpost = np.ascontiguousarray(np.broadcast_to(np.asarray(norm_post, f)[:, None, :], (2, 128, D)))
    ktab, cst, cbf = _const_tables()

    if "nc" not in _CACHE:
        _CACHE["nc"] = build_program()[0]
    nc = _CACHE["nc"]
    in_maps = []
    for c in range(8):
        bs = slice(2 * c, 2 * c + 2)
        ccT = np.ascontiguousarray(cache_conv[:, bs].reshape(2, 2, CS, 8, 128).transpose(0, 1, 4, 3, 2))
        in_maps.append({
            "xp": np.ascontiguousarray(x_prompt[c]),
            "xs": np.ascontiguousarray(x_sample[bs].reshape(128, D)),
            "wall": wall, "pvec": pvec, "wpost": wpost, "ccT": ccT,
            "sret": np.ascontiguousarray(state_ret[:, bs]),
            "ktab": ktab, "cstab": cst, "cbf": cbf,
        })
    res = run_bass_kernel_spmd(nc, in_maps, core_ids=list(range(8)))
    rs = res.results
    y_prompt = np.stack([rs[c]["yp"] for c in range(8)], 0)
    y_sample = np.concatenate([rs[c]["ys"].reshape(2, DS, D) for c in range(8)], 0)
    ncp = np.stack([rs[c]["ncp"] for c in range(8)], 1)
    nrp = np.stack([rs[c]["nrp"] for c in range(8)], 1)
    ncs = np.concatenate([rs[c]["ncs"] for c in range(8)], 1)
    nrs = np.concatenate([rs[c]["nrs"] for c in range(8)], 1)
    return (y_prompt.astype(f), y_sample.astype(f), ncp.astype(f), nrp.astype(f), ncs.astype(f), nrs.astype(f))
```
